# Optimizing a Trainium2 kernel written in Bass

```python
import math
import jax, jax.numpy as jnp
from jax import lax
import numpy as np

D_MODEL = 2048
BATCH = 8
SEQ = 4096
DEPTH = 4

DN_HEADS = 16
DN_HEAD_DIM = 128
DN_WIDTH = DN_HEADS * DN_HEAD_DIM
DN_CONV = 4
DN_CHUNK = 64
DA_PATTERNS = ((128, 1), (512, 4), (2048, 16))
DA_GROUPS = len(DA_PATTERNS)
DA_HEADS_PER_GROUP = 8
DA_HEAD_DIM = 128
DA_WIDTH = DA_HEADS_PER_GROUP * DA_HEAD_DIM
ROPE_THETA = 10000.0
N_EXPERTS = 32
TOP_K = 4
D_EXPERT = 512
SWIGLU_LIMIT = 7.0
SWIGLU_ALPHA = 1.702
DEEPNORM_ALPHA = (2 * DEPTH) ** 0.25
DEEPNORM_BETA = (8 * DEPTH) ** -0.25
LN_EPS = 1e-5
NORM_EPS = 1e-6
IN_SIZES = (3 * DN_WIDTH, DN_WIDTH, DN_HEADS, DN_HEADS, 3 * DA_GROUPS * DA_WIDTH, D_MODEL, D_MODEL)
IN_WIDTH = sum(IN_SIZES)

kernel_name = "hybrid_gdn_dilated_moe_deepnorm"


def layer_norm(x, g, b):
    xf = x.astype(jnp.float32)
    mu = jnp.mean(xf, axis=-1, keepdims=True)
    var = jnp.mean(jnp.square(xf - mu), axis=-1, keepdims=True)
    return ((xf - mu) * lax.rsqrt(var + LN_EPS) * g.astype(jnp.float32) + b.astype(jnp.float32)).astype(x.dtype)


def l2_normalize(t):
    return t * lax.rsqrt(jnp.sum(t * t, axis=-1, keepdims=True) + NORM_EPS)


def rotary(t, positions):
    half = t.shape[-1] // 2
    inv_freq = ROPE_THETA ** (-jnp.arange(half, dtype=jnp.float32) / half)
    ang = positions.astype(jnp.float32)[:, None] * inv_freq[None, :]
    cos = jnp.cos(ang)[None, :, None, :]
    sin = jnp.sin(ang)[None, :, None, :]
    tf = t.astype(jnp.float32)
    t1, t2 = tf[..., :half], tf[..., half:]
    return jnp.concatenate([t1 * cos - t2 * sin, t2 * cos + t1 * sin], axis=-1).astype(t.dtype)


def causal_depthwise_conv(x, w):
    K, C = w.shape
    return lax.conv_general_dilated(x, w[:, None, :].astype(x.dtype), window_strides=(1,),
                                    padding=[(K - 1, 0)], dimension_numbers=("NWC", "WIO", "NWC"),
                                    feature_group_count=C)


def gated_delta_rule_chunked(q, k, v, g, beta):
    B, S, H, Dk = q.shape
    Dv = v.shape[-1]
    C = DN_CHUNK
    N = S // C

    def chunks(t):
        return t.reshape(B, N, C, H, t.shape[-1]).transpose(0, 3, 1, 2, 4)

    q, k, v = chunks(q), chunks(k), chunks(v)
    beta = beta.reshape(B, N, C, H).transpose(0, 3, 1, 2)
    g = jnp.cumsum(g.reshape(B, N, C, H).transpose(0, 3, 1, 2), axis=-1)
    causal = jnp.tril(jnp.ones((C, C), dtype=bool))
    strict = jnp.tril(jnp.ones((C, C), dtype=bool), -1)
    decay = jnp.exp(jnp.where(causal, g[..., :, None] - g[..., None, :], -jnp.inf))
    k_beta = k * beta[..., None]
    lower = jnp.where(strict, jnp.einsum("bhncd,bhnmd->bhncm", k_beta, k) * decay, 0.0)
    eye = jnp.eye(C, dtype=jnp.float32)
    unit = eye + lower
    t_inv = lax.linalg.triangular_solve(unit, jnp.broadcast_to(eye, unit.shape), left_side=True,
                                        lower=True, unit_diagonal=True)
    u = jnp.einsum("bhncm,bhnmv->bhncv", t_inv, v * beta[..., None])
    w = jnp.einsum("bhncm,bhnmk->bhnck", t_inv, k_beta * jnp.exp(g)[..., None])
    intra = jnp.einsum("bhncd,bhnmd->bhncm", q, k) * decay

    def step(state, xs):
        q_c, k_c, u_c, w_c, g_c, a_c = xs
        v_new = u_c - jnp.einsum("bhck,bhkv->bhcv", w_c, state)
        o = (jnp.einsum("bhck,bhkv->bhcv", q_c * jnp.exp(g_c)[..., None], state)
             + jnp.einsum("bhcm,bhmv->bhcv", a_c, v_new))
        g_last = g_c[..., -1]
        state = (state * jnp.exp(g_last)[..., None, None]
                 + jnp.einsum("bhck,bhcv->bhkv", k_c * jnp.exp(g_last[..., None] - g_c)[..., None], v_new))
        return state, o

    xs = tuple(jnp.moveaxis(t, 2, 0) for t in (q, k, u, w, g, intra))
    state0 = jnp.zeros((B, H, Dk, Dv), jnp.float32)
    _, o = lax.scan(step, state0, xs)
    return o.transpose(1, 0, 3, 2, 4).reshape(B, S, H, Dv)


def dilated_window_attention(q, k, v, window, dilation):
    B, S, H, Dh = q.shape
    L = window // dilation
    span = dilation * L
    S_pad = -(-S // span) * span
    n = S_pad // dilation
    nb = n // L

    def to_streams(t):
        t = jnp.pad(t, ((0, 0), (0, S_pad - S), (0, 0), (0, 0)))
        return t.reshape(B, n, dilation, H, Dh).transpose(0, 3, 2, 1, 4).reshape(B, H, dilation, nb, L, Dh)

    def with_prev(t):
        prev = jnp.pad(t, ((0, 0), (0, 0), (0, 0), (1, 0), (0, 0), (0, 0)))[:, :, :, :-1]
        return jnp.concatenate([prev, t], axis=4)

    qs = to_streams(q)
    kw = with_prev(to_streams(k))
    vw = with_prev(to_streams(v)).astype(jnp.float32)
    s = jnp.einsum("bhrnqd,bhrnkd->bhrnqk", qs, kw).astype(jnp.float32) * (Dh ** -0.5)
    a = jnp.arange(L)[:, None]
    c = jnp.arange(2 * L)[None, :]
    band = (c >= a) & (c <= a + L)
    blk = jnp.arange(nb)[:, None, None]
    mask = band[None] & ((blk > 0) | (c >= L)[None])
    s = jnp.where(mask, s, -jnp.inf)
    lse = jax.nn.logsumexp(s, axis=-1)
    p = jnp.exp(s - lse[..., None])
    o = jnp.einsum("bhrnqk,bhrnkd->bhrnqd", p, vw)
    o = o.reshape(B, H, dilation, n, Dh).transpose(0, 3, 2, 1, 4).reshape(B, S_pad, H, Dh)[:, :S]
    lse = lse.reshape(B, H, dilation, n).transpose(0, 3, 2, 1).reshape(B, S_pad, H)[:, :S]
    return o, lse


def token_mixer(h, w_in, conv_w, a_log, dt_bias, dn_norm_w, w_branch_a, w_branch_b, w_out):
    B, S, _ = h.shape
    proj = h @ w_in
    offsets = np.cumsum(IN_SIZES)[:-1].tolist()
    qkv_a, z, a_lin, b_lin, qkv_b, gate_a, gate_b = jnp.split(proj, offsets, axis=-1)

    qkv_a = jax.nn.silu(causal_depthwise_conv(qkv_a, conv_w)).astype(jnp.float32)
    qa, ka, va = jnp.split(qkv_a.reshape(B, S, 3, DN_HEADS, DN_HEAD_DIM), 3, axis=2)
    qa = l2_normalize(qa[:, :, 0]) * (DN_HEAD_DIM ** -0.5)
    ka = l2_normalize(ka[:, :, 0])
    va = va[:, :, 0]
    beta = jax.nn.sigmoid(b_lin.astype(jnp.float32))
    g = -jnp.exp(a_log.astype(jnp.float32)) * jax.nn.softplus(a_lin.astype(jnp.float32) + dt_bias.astype(jnp.float32))
    oa = gated_delta_rule_chunked(qa, ka, va, g, beta)
    oa = oa * lax.rsqrt(jnp.mean(oa * oa, axis=-1, keepdims=True) + NORM_EPS) * dn_norm_w.astype(jnp.float32)
    oa = oa * jax.nn.silu(z.astype(jnp.float32).reshape(B, S, DN_HEADS, DN_HEAD_DIM))
    y_a = oa.reshape(B, S, DN_WIDTH).astype(h.dtype)

    qkv_b = qkv_b.reshape(B, S, 3, DA_GROUPS * DA_HEADS_PER_GROUP, DA_HEAD_DIM)
    pos = jnp.arange(S)
    qb = rotary(qkv_b[:, :, 0], pos)
    kb = rotary(qkv_b[:, :, 1], pos)
    vb = qkv_b[:, :, 2]
    outs, lses = [], []
    for gi, (win, dil) in enumerate(DA_PATTERNS):
        hs = slice(gi * DA_HEADS_PER_GROUP, (gi + 1) * DA_HEADS_PER_GROUP)
        o_g, l_g = dilated_window_attention(qb[:, :, hs], kb[:, :, hs], vb[:, :, hs], win, dil)
        outs.append(o_g)
        lses.append(l_g)
    wts = jax.nn.softmax(jnp.stack(lses, axis=0), axis=0)
    y_b = jnp.einsum("gbsh,gbshd->bshd", wts, jnp.stack(outs, axis=0))
    y_b = y_b.reshape(B, S, DA_WIDTH).astype(h.dtype)

    merged = jax.nn.sigmoid(gate_a) * (y_a @ w_branch_a) + jax.nn.sigmoid(gate_b) * (y_b @ w_branch_b)
    return merged @ w_out


def moe_ffn(h, router_w, router_b, w_gate_up, b_gate_up, w_down, b_down):
    B, S, D = h.shape
    t = h.reshape(B * S, D)
    logits = (t @ router_w + router_b).astype(jnp.float32)
    top_val, top_idx = lax.top_k(logits, TOP_K)
    top_w = jax.nn.softmax(top_val, axis=-1)
    combine = jnp.einsum("tk,tke->te", top_w, jax.nn.one_hot(top_idx, N_EXPERTS, dtype=jnp.float32))
    out = jnp.zeros((B * S, D), jnp.float32)
    for e in range(N_EXPERTS):
        gu = t @ w_gate_up[e] + b_gate_up[e]
        gate = jnp.minimum(gu[:, :D_EXPERT], SWIGLU_LIMIT)
        up = jnp.clip(gu[:, D_EXPERT:], -SWIGLU_LIMIT, SWIGLU_LIMIT)
        glu = gate * jax.nn.sigmoid(gate * SWIGLU_ALPHA)
        y = ((up + 1.0) * glu) @ w_down[e] + b_down[e]
        out = out + combine[:, e:e + 1] * y
    return out.astype(h.dtype).reshape(B, S, D)


def setup_inputs(seed: int = 0) -> dict:
    key = jax.random.key(seed)
    ks = jax.random.split(key, 20)
    f32 = jnp.float32
    D = D_MODEL

    def nrm(k, shape, scale):
        return jax.random.normal(k, shape, f32) * scale

    x = nrm(ks[0], (BATCH, SEQ, D), 1.0)
    col_scale = jnp.concatenate([
        jnp.ones((2 * DN_WIDTH,), f32), jnp.full((DN_WIDTH,), DEEPNORM_BETA, f32),
        jnp.ones((DN_WIDTH + 2 * DN_HEADS,), f32),
        jnp.ones((2 * DA_GROUPS * DA_WIDTH,), f32), jnp.full((DA_GROUPS * DA_WIDTH,), DEEPNORM_BETA, f32),
        jnp.ones((2 * D,), f32)])
    w_in = nrm(ks[1], (DEPTH, D, IN_WIDTH), D ** -0.5) * col_scale
    conv_w = nrm(ks[2], (DEPTH, DN_CONV, 3 * DN_WIDTH), DN_CONV ** -0.5)
    a_log = jnp.log(jax.random.uniform(ks[3], (DEPTH, DN_HEADS), f32, 1.0, 16.0))
    dt = jnp.exp(jax.random.uniform(ks[4], (DEPTH, DN_HEADS), f32, math.log(1e-3), math.log(1e-1)))
    dt_bias = dt + jnp.log(-jnp.expm1(-dt))
    dn_norm_w = 1.0 + nrm(ks[5], (DEPTH, DN_HEAD_DIM), 0.02)
    w_branch_a = nrm(ks[6], (DEPTH, DN_WIDTH, D), DN_WIDTH ** -0.5)
    w_branch_b = nrm(ks[7], (DEPTH, DA_WIDTH, D), DA_WIDTH ** -0.5)
    w_out = nrm(ks[8], (DEPTH, D, D), D ** -0.5 * DEEPNORM_BETA)
    ln1_g = 1.0 + nrm(ks[9], (DEPTH, D), 0.02)
    ln1_b = nrm(ks[10], (DEPTH, D), 0.02)
    router_w = nrm(ks[11], (DEPTH, D, N_EXPERTS), D ** -0.5)
    router_b = nrm(ks[12], (DEPTH, N_EXPERTS), 0.01)
    w_gate_up = nrm(ks[13], (DEPTH, N_EXPERTS, D, 2 * D_EXPERT), D ** -0.5 * DEEPNORM_BETA)
    b_gate_up = nrm(ks[14], (DEPTH, N_EXPERTS, 2 * D_EXPERT), 0.01)
    w_down = nrm(ks[15], (DEPTH, N_EXPERTS, D_EXPERT, D), D_EXPERT ** -0.5 * DEEPNORM_BETA)
    b_down = nrm(ks[16], (DEPTH, N_EXPERTS, D), 0.01)
    ln2_g = 1.0 + nrm(ks[17], (DEPTH, D), 0.02)
    ln2_b = nrm(ks[18], (DEPTH, D), 0.02)
    return {"x": x, "w_in": w_in, "conv_w": conv_w, "a_log": a_log, "dt_bias": dt_bias,
            "dn_norm_w": dn_norm_w, "w_branch_a": w_branch_a, "w_branch_b": w_branch_b, "w_out": w_out,
            "ln1_g": ln1_g, "ln1_b": ln1_b, "router_w": router_w, "router_b": router_b,
            "w_gate_up": w_gate_up, "b_gate_up": b_gate_up, "w_down": w_down, "b_down": b_down,
            "ln2_g": ln2_g, "ln2_b": ln2_b}


def reference(x, w_in, conv_w, a_log, dt_bias, dn_norm_w, w_branch_a, w_branch_b, w_out,
              ln1_g, ln1_b, router_w, router_b, w_gate_up, b_gate_up, w_down, b_down, ln2_g, ln2_b):
    for l in range(DEPTH):
        mix = token_mixer(x, w_in[l], conv_w[l], a_log[l], dt_bias[l], dn_norm_w[l],
                          w_branch_a[l], w_branch_b[l], w_out[l])
        x = layer_norm(DEEPNORM_ALPHA * x + mix, ln1_g[l], ln1_b[l])
        ffn = moe_ffn(x, router_w[l], router_b[l], w_gate_up[l], b_gate_up[l], w_down[l], b_down[l])
        x = layer_norm(DEEPNORM_ALPHA * x + ffn, ln2_g[l], ln2_b[l])
    return x
```

```python
import contextlib
import itertools
import math

import numpy as np
import concourse.bass as bass
import concourse.mybir as mybir
from concourse.bass_utils import run_bass_kernel_spmd

F32 = mybir.dt.float32
BF16 = mybir.dt.bfloat16
AF = mybir.ActivationFunctionType
ALU = mybir.AluOpType
AX = mybir.AxisListType

NEG = -30000.0
DBG = {}


class Cfg:
    def __init__(self, D=2048, S=4096, L=4, DNH=16, DAH=8, NE=32, DE=512, alpha=8 ** 0.25,
                 pats=((128, 1), (512, 4), (2048, 16))):
        self.D, self.S, self.L, self.DNH, self.DAH, self.NE, self.DE = D, S, L, DNH, DAH, NE, DE
        self.alpha = alpha
        self.pats = pats
        self.KC = D // 128
        self.DNW = DNH * 128
        self.DAW = DAH * 128
        self.NG = len(pats)
        self.sizes = (3 * self.DNW, self.DNW, DNH, DNH, 3 * self.NG * self.DAW, D, D)
        self.INW = sum(self.sizes)
        o = np.cumsum((0,) + self.sizes)
        self.oQKVA, self.oZ, self.oA, self.oB, self.oQKVB, self.oGA, self.oGB = [int(v) for v in o[:7]]
        self.NCH = S // 128
        self.TOPK = 4


class TK:
    __slots__ = ("w", "r")

    def __init__(self):
        self.w = None
        self.r = {}


class EngW:
    def __init__(self, kb, eng, name, is_pe=False):
        self.eng = eng
        self.name = name
        self.is_pe = is_pe
        self.sem = kb.nc.alloc_semaphore("prog_" + name)
        self.key = "E" + name
        self.cnt = 0
        self.sval = 0
        self.seen = {}
        self.dsems = []
        self.dvals = []
        self.dnext = 0


class KB:
    def __init__(self, nc, ndma=6, needed=None):
        self.nc = nc
        self.needed = needed
        self.rec = {}
        self.pe = EngW(self, nc.tensor, "pe", True)
        self.act = EngW(self, nc.scalar, "act")
        self.dve = EngW(self, nc.vector, "dve")
        self.pool = EngW(self, nc.gpsimd, "pool")
        self.sp = EngW(self, nc.sync, "sp")
        self.all = [self.pe, self.act, self.dve, self.pool, self.sp]
        for e in (self.sp, self.act, self.pool):
            for i in range(ndma):
                e.dsems.append(nc.alloc_semaphore("dma_%s_%d" % (e.name, i)))
                e.dvals.append(0)
        self.uid = 0

    def _wait(self, e, deps):
        for (sem, val, key) in deps:
            if key == e.key and (e.is_pe or DBG.get("no_self_wait", False)):
                continue
            if e.seen.get(key, 0) < val:
                e.eng.wait_ge(sem, val)
                e.seen[key] = val
                if key[0] == "E":
                    self.rec.setdefault(key, set()).add(val)

    def _tok(self, e, ins):
        e.cnt += 1
        if self.needed is None:
            ins.then_inc(e.sem, 1)
            return (e.sem, e.cnt, e.key)
        if e.cnt in self.needed.get(e.key, ()):
            ins.then_inc(e.sem, 1)
            e.sval += 1
        return (e.sem, e.sval, e.key)

    def _deps(self, R, W):
        deps = []
        for t in R:
            if t.w is not None:
                deps.append(t.w)
        for t in W:
            if t.w is not None:
                deps.append(t.w)
            deps.extend(t.r.values())
        return deps

    def _commit(self, tok, R, W):
        for t in W:
            t.w = tok
            t.r = {}
        for t in R:
            if t not in W:
                t.r[tok[2]] = tok

    def op(self, e, fn, R=(), W=()):
        self._wait(e, self._deps(R, W))
        ins = fn()
        tok = self._tok(e, ins)
        self._commit(tok, R, W)
        return tok

    def mm(self, out, pairs, R=(), W=(), fp32=False):
        e = self.pe
        self._wait(e, self._deps(R, W))
        n = len(pairs)
        ins = None
        for i, (l, r) in enumerate(pairs):
            ins = self.nc.tensor.matmul(out, l, r, start=(i == 0), stop=(i == n - 1))
        tok = self._tok(e, ins)
        self._commit(tok, R, W)
        return tok

    def dma(self, q, out, in_, R=(), W=(), nc_ok=False):
        i = q.dnext
        q.dnext = (q.dnext + 1) % len(q.dsems)
        sem = q.dsems[i]
        key = "D%s%d" % (q.name, i)
        deps = self._deps(R, W)
        if q.dvals[i] > 0:
            deps.append((sem, q.dvals[i], key))
        self._wait(q, deps)
        if nc_ok:
            with self.nc.allow_non_contiguous_dma(reason="small strided"):
                ins = q.eng.dma_start(out=out, in_=in_)
        else:
            ins = q.eng.dma_start(out=out, in_=in_)
        q.dvals[i] += 16
        ins.then_inc(sem, 16)
        tok = (sem, q.dvals[i], key)
        self._commit(tok, R, W)
        return tok

    def barrier(self):
        toks = [(e.sem, e.cnt if self.needed is None else e.sval, e.key) for e in self.all if e.cnt > 0]
        for q in (self.sp, self.act, self.pool):
            for i, s in enumerate(q.dsems):
                if q.dvals[i] > 0:
                    toks.append((s, q.dvals[i], "D%s%d" % (q.name, i)))
        for e in self.all:
            for (sem, val, key) in toks:
                if key == e.key:
                    continue
                if e.seen.get(key, 0) < val:
                    e.eng.wait_ge(sem, val)
                    e.seen[key] = val
                    if key[0] == "E":
                        self.rec.setdefault(key, set()).add(val)


class Pool:
    def __init__(self, tiles):
        self.tiles = [(t, TK()) for t in tiles]
        self.i = 0

    def next(self):
        t = self.tiles[self.i]
        self.i = (self.i + 1) % len(self.tiles)
        return t


def build_program(cfg, debug_outs=(), phases=None, needed=None, ret_rec=False):
    c = cfg
    nc = bass.Bass("TRN2", target_bir_lowering=False)
    D, S, L, KC, NCH = c.D, c.S, c.L, c.KC, c.NCH
    DNH, DAH, NE, DE, NG = c.DNH, c.DAH, c.NE, c.DE, c.NG
    NT5 = S // 512
    NHB = NG * DAH
    EC = DE // 128
    NUQ = 3 * DNH

    def din(name, shape):
        return nc.dram_tensor(name, list(shape), F32, kind="ExternalInput").ap()

    x_in = din("x", [S, D])
    w_in = din("w_in", [L, D, c.INW])
    conv_w = din("conv_w", [L, 128, NUQ, 4])
    a_log = din("a_log", [L, DNH, 1])
    dt_bias = din("dt_bias", [L, DNH, 1])
    dn_norm_w = din("dn_norm_w", [L, 128])
    w_ba = din("w_branch_a", [L, c.DNW, D])
    w_bb = din("w_branch_b", [L, c.DAW, D])
    w_out = din("w_out", [L, D, D])
    ln1_g = din("ln1_g", [L, D])
    ln1_b = din("ln1_b", [L, D])
    router_w = din("router_w", [L, D, NE])
    router_b = din("router_b", [L, NE])
    w_gu = din("w_gate_up", [L, NE, D, 2 * DE])
    b_gu = din("b_gate_up", [L, 128, NE * 2 * EC])
    w_dn = din("w_down", [L, NE, DE, D])
    b_dn = din("b_down", [L, NE, D])
    ln2_g = din("ln2_g", [L, D])
    ln2_b = din("ln2_b", [L, D])
    consts = din("consts", [128, 6 * 128])
    amask_in = din("amask", [128, 256])
    cos_in = din("cos_t", [128, S])
    sin_in = din("sin_t", [128, S])
    scanmask_in = din("scanmask", [DNH, S])
    sele_in = din("sele", [NE, NE * 128])
    y_out = nc.dram_tensor("y", [S, D], F32, kind="ExternalOutput").ap()

    dbg = {}

    def dscr(name, shape, dt):
        kind = "ExternalOutput" if name in debug_outs else "Internal"
        t = nc.dram_tensor(name, list(shape), dt, kind=kind).ap()
        dbg[name] = t
        return t

    xa = dscr("xa", [S, D], F32)
    xb = dscr("xb", [S, D], F32)
    qkvA = dscr("qkvA", [NUQ, 128, S], BF16)
    zT = dscr("zT", [DNH, 128, S], BF16)
    ab_d = dscr("ab_d", [2, DNH, S], F32)
    qB = dscr("qB", [NHB, 128, S], BF16)
    kBd = dscr("kB", [NHB, 128, S], BF16)
    vB = dscr("vB", [NHB, 128, NCH * 128], BF16)
    gAd = dscr("gA", [KC, 128, S], BF16)
    gBd = dscr("gB", [KC, 128, S], BF16)
    yAd = dscr("yA", [DNH, 128, S], BF16)
    yBd = dscr("yB", [DAH, 128, S], BF16)
    mrg = dscr("mrg", [KC, 128, S], BF16)
    hT = dscr("hT", [NE * EC, 128, S], BF16)
    combT_d = dscr("combT", [NE, S], F32)

    kb = KB(nc, needed=needed)
    PE, ACT, DVE, POOL, SP = kb.pe, kb.act, kb.dve, kb.pool, kb.sp
    NDUMP = 40
    dumpT = dscr("dumpT", [NDUMP, 128, 128], F32) if "dumpT" in debug_outs else None
    dump_names = []
    dump_tk = TK()

    def dump(name, ap, ap_tk, np_=128, nf=128):
        if dumpT is None:
            return
        i = len(dump_names)
        dump_names.append(name)
        DBG.setdefault("dump_names", []).append(name)
        st_ = dump_st[i % 2]
        kb.op(DVE, lambda: nc.vector.memset(st_[0][:], 0.0), W=[st_[1]])
        kb.op(DVE, lambda: nc.vector.tensor_scalar(out=st_[0][0:np_, 0:nf], in0=ap, scalar1=-1e30, scalar2=1e30,
                                                   op0=ALU.max, op1=ALU.min), R=[ap_tk], W=[st_[1]])
        kb.dma(SP, dumpT[i], st_[0][:], R=[st_[1]], W=[dump_tk])
    V, A, G, T = nc.vector, nc.scalar, nc.gpsimd, nc.tensor

    tk = {n: TK() for n in ("xa", "xb", "qkvA", "zT", "ab", "qB", "kB", "vB", "gA", "gB", "yA", "yB", "mrg",
                            "hT", "combT", "y")}

    with contextlib.ExitStack() as top:
        uid = [0]

        def sb(es, name, shape, dt=F32):
            uid[0] += 1
            return es.enter_context(nc.sbuf_tensor("%s_%d" % (name, uid[0]), list(shape), dt))

        def ps(es, name, shape, dt=F32):
            uid[0] += 1
            return es.enter_context(nc.psum_tensor("%s_%d" % (name, uid[0]), list(shape), dt))

        cst = sb(top, "cst", [128, 6 * 128])
        cst_tk = TK()
        kb.dma(SP, cst[:], consts[:, :], W=[cst_tk])
        ident_f = cst[:, 0:128]
        ones_f = cst[:, 128:256]
        maskS = cst[:, 256:384]
        maskT = cst[:, 384:512]
        rmat = cst[:, 512:640]
        sel127 = cst[:, 640:768]
        cstb = sb(top, "cstb", [128, 2 * 128], BF16)
        cstb_tk = TK()
        kb.op(DVE, lambda: V.tensor_copy(out=cstb[:], in_=cst[:, 0:256]), R=[cst_tk], W=[cstb_tk])
        ident_b = cstb[:, 0:128]
        ones_b = cstb[:, 128:256]
        dump_st = [(sb(top, "dump_st%d" % i, [128, 128]), TK()) for i in range(2)]
        CT = [cst_tk, cstb_tk]

        def phase_transpose(es, src, src_tk):
            xT = sb(es, "xT", [128, KC, S], BF16)
            xT_tk = [TK() for _ in range(NT5)]
            with contextlib.ExitStack() as es2:
                xin = Pool([sb(es2, "xin%d" % i, [128, D]) for i in range(2)])
                pst = Pool([ps(es2, "pst%d" % i, [128, 512]) for i in range(4)])
                flip = 0
                for tt in range(S // 128):
                    xi, xi_tk = xin.next()
                    kb.dma(SP, xi[:], src[tt * 128:(tt + 1) * 128, :], R=[src_tk], W=[xi_tk])
                    for k4 in range((KC + 3) // 4):
                        nk = min(4, KC - k4 * 4)
                        p, p_tk = pst.next()
                        for j in range(nk):
                            kk = k4 * 4 + j
                            kb.op(PE, lambda j=j, kk=kk: T.transpose(p[:, j * 128:(j + 1) * 128],
                                                                     xi[:, kk * 128:(kk + 1) * 128], ident_f),
                                  R=[xi_tk, cst_tk], W=[p_tk])
                        o = xT[:, k4 * 4:k4 * 4 + nk, tt * 128:(tt + 1) * 128]
                        i_ = p[:, 0:nk * 128].rearrange("p (j t) -> p j t", j=nk)
                        if flip:
                            kb.op(DVE, lambda: V.tensor_copy(out=o, in_=i_), R=[p_tk], W=[xT_tk[tt // 4]])
                        else:
                            kb.op(ACT, lambda: A.copy(out=o, in_=i_), R=[p_tk], W=[xT_tk[tt // 4]])
                        flip ^= 1
                kb.barrier()
            return xT, xT_tk

        def phase_proj(l, src, src_tk):
            with contextlib.ExitStack() as es:
                xT, xT_tk = phase_transpose(es, src, src_tk)
                wst = Pool([sb(es, "wst%d" % i, [128, KC, 128]) for i in range(2)])
                wbf = Pool([sb(es, "wbf%d" % i, [128, KC, 128], BF16) for i in range(2)])
                stg = Pool([sb(es, "stg%d" % i, [128, S], BF16) for i in range(2)])
                stfp = Pool([sb(es, "stf%d" % i, [DNH, 512]) for i in range(2)])
                cw = sb(es, "cw", [128, NUQ, 4])
                cw_tk = TK()
                kb.dma(SP, cw[:], conv_w[l], W=[cw_tk])
                raw = Pool([sb(es, "raw%d" % i, [128, 3 + 512]) for i in range(2)])
                acc = Pool([sb(es, "acc%d" % i, [128, 512]) for i in range(2)])
                t1p = Pool([sb(es, "t1p%d" % i, [128, 512]) for i in range(2)])
                t2p = Pool([sb(es, "t2p%d" % i, [128, 512]) for i in range(2)])
                csp = Pool([sb(es, "csp%d" % i, [128, 2, 512]) for i in range(2)])
                pp = Pool([ps(es, "pp%d" % i, [128, 512]) for i in range(5)])
                pq = Pool([ps(es, "pq%d" % i, [128, 512]) for i in range(2)])

                def load_w(col0, ncols):
                    ws, ws_tk = wst.next()
                    kb.dma(SP, ws[:, :, 0:ncols],
                           w_in[l, :, col0:col0 + ncols].rearrange("(k p) c -> p k c", p=128), W=[ws_tk])
                    wb, wb_tk = wbf.next()
                    kb.op(POOL, lambda: G.tensor_copy(out=wb[:, :, 0:ncols], in_=ws[:, :, 0:ncols]),
                          R=[ws_tk], W=[wb_tk])
                    return wb, wb_tk

                def fm_unit(col0, ncols, epi):
                    wb, wb_tk = load_w(col0, ncols)
                    for tt in range(NT5):
                        p, p_tk = pp.next()
                        kb.mm(p[0:ncols, :], [(wb[:, k, 0:ncols], xT[:, k, tt * 512:(tt + 1) * 512])
                                              for k in range(KC)], R=[wb_tk, xT_tk[tt]], W=[p_tk])
                        epi(p, p_tk, tt)

                for u in range(NUQ):
                    kind = u // DNH
                    st, st_tk = stg.next()
                    prev = [None]

                    def epi(p, p_tk, tt, u=u, kind=kind, st=st, st_tk=st_tk, prev=prev):
                        rw, rw_tk = raw.next()
                        if tt == 0:
                            kb.op(DVE, lambda: V.memset(rw[:, 0:3], 0.0), W=[rw_tk])
                        else:
                            pr, pr_tk = prev[0]
                            kb.op(DVE, lambda: V.tensor_copy(out=rw[:, 0:3], in_=pr[:, 512:515]), R=[pr_tk],
                                  W=[rw_tk])
                        kb.op(ACT, lambda: A.copy(out=rw[:, 3:515], in_=p[:, :]), R=[p_tk], W=[rw_tk])
                        prev[0] = (rw, rw_tk)
                        ac, ac_tk = acc.next()
                        kb.op(DVE, lambda: V.tensor_scalar(out=ac[:], in0=rw[:, 0:512], scalar1=cw[:, u, 0:1],
                                                           scalar2=None, op0=ALU.mult), R=[rw_tk, cw_tk], W=[ac_tk])
                        for j in (1, 2, 3):
                            kb.op(DVE, lambda j=j: V.scalar_tensor_tensor(out=ac[:], in0=rw[:, j:j + 512],
                                                                          scalar=cw[:, u, j:j + 1], in1=ac[:],
                                                                          op0=ALU.mult, op1=ALU.add),
                                  R=[rw_tk, cw_tk, ac_tk], W=[ac_tk])
                        if kind == 2:
                            kb.op(ACT, lambda: A.activation(out=st[:, tt * 512:(tt + 1) * 512], in_=ac[:],
                                                            func=AF.Silu), R=[ac_tk], W=[st_tk])
                            return
                        s1, s1_tk = t1p.next()
                        kb.op(ACT, lambda: A.activation(out=s1[:], in_=ac[:], func=AF.Silu), R=[ac_tk], W=[s1_tk])
                        s2, s2_tk = t2p.next()
                        kb.op(DVE, lambda: V.tensor_tensor(out=s2[:], in0=s1[:], in1=s1[:], op=ALU.mult),
                              R=[s1_tk], W=[s2_tk])
                        q_, q_tk = pq.next()
                        kb.mm(q_[:, :], [(ones_f, s2[:])], R=[s2_tk, cst_tk], W=[q_tk])
                        kb.op(ACT, lambda: A.activation(out=s2[:], in_=q_[:, :], func=AF.Ln, bias=1e-6, scale=1.0),
                              R=[q_tk], W=[s2_tk])
                        kb.op(ACT, lambda: A.activation(out=s2[:], in_=s2[:], func=AF.Exp, scale=-0.5),
                              R=[s2_tk], W=[s2_tk])
                        sc = (128.0 ** -0.5) if kind == 0 else 1.0
                        kb.op(DVE, lambda: V.scalar_tensor_tensor(out=st[:, tt * 512:(tt + 1) * 512], in0=s1[:],
                                                                  scalar=sc, in1=s2[:], op0=ALU.mult, op1=ALU.mult),
                              R=[s1_tk, s2_tk], W=[st_tk])

                    fm_unit(c.oQKVA + u * 128, 128, epi)
                    kb.dma(ACT, qkvA[u], st[:], R=[st_tk], W=[tk["qkvA"]])

                for (col0, nun, dst, dkey, fn) in ((c.oZ, DNH, zT, "zT", AF.Silu), (c.oGA, KC, gAd, "gA", AF.Sigmoid),
                                                   (c.oGB, KC, gBd, "gB", AF.Sigmoid)):
                    for u in range(nun):
                        st, st_tk = stg.next()

                        def epi(p, p_tk, tt, st=st, st_tk=st_tk, fn=fn):
                            kb.op(ACT, lambda: A.activation(out=st[:, tt * 512:(tt + 1) * 512], in_=p[:, :], func=fn),
                                  R=[p_tk], W=[st_tk])

                        fm_unit(col0 + u * 128, 128, epi)
                        kb.dma(ACT, dst[u], st[:], R=[st_tk], W=[tk[dkey]])

                for i, col0 in enumerate((c.oA, c.oB)):
                    def epi(p, p_tk, tt, i=i):
                        sf, sf_tk = stfp.next()
                        kb.op(ACT, lambda: A.copy(out=sf[:], in_=p[0:DNH, :]), R=[p_tk], W=[sf_tk])
                        kb.dma(ACT, ab_d[i][:, tt * 512:(tt + 1) * 512], sf[:], R=[sf_tk], W=[tk["ab"]])

                    fm_unit(col0, DNH, epi)

                for qk in range(2):
                    for hd in range(NHB):
                        g = hd // DAH
                        dil = c.pats[g][1]
                        st, st_tk = stg.next()

                        def epi(p, p_tk, tt, st=st, st_tk=st_tk, dil=dil):
                            cs, cs_tk = csp.next()
                            kb.dma(SP, cs[:, 0, :], cos_in[:, tt * 512:(tt + 1) * 512], W=[cs_tk])
                            kb.dma(SP, cs[:, 1, :], sin_in[:, tt * 512:(tt + 1) * 512], W=[cs_tk])
                            t1, t1_tk = t1p.next()
                            kb.op(ACT, lambda: A.copy(out=t1[:], in_=p[:, :]), R=[p_tk], W=[t1_tk])
                            q_, q_tk = pq.next()
                            kb.mm(q_[:, :], [(rmat, t1[:])], R=[t1_tk, cst_tk], W=[q_tk])
                            t2, t2_tk = t2p.next()
                            kb.op(DVE, lambda: V.tensor_tensor(out=t2[:], in0=q_[:, :], in1=cs[:, 1, :], op=ALU.mult),
                                  R=[q_tk, cs_tk], W=[t2_tk])
                            kb.op(POOL, lambda: G.tensor_tensor(out=t1[:], in0=t1[:], in1=cs[:, 0, :], op=ALU.mult),
                                  R=[t1_tk, cs_tk], W=[t1_tk])
                            npr = 512 // dil
                            o = st[:, :].rearrange("p (r n) -> p r n", r=dil)[:, :, tt * npr:(tt + 1) * npr]
                            i0 = t1[:, :].rearrange("p (i r) -> p r i", r=dil)
                            i1 = t2[:, :].rearrange("p (i r) -> p r i", r=dil)
                            kb.op(DVE, lambda: V.tensor_tensor(out=o, in0=i0, in1=i1, op=ALU.add),
                                  R=[t1_tk, t2_tk], W=[st_tk])

                        fm_unit(c.oQKVB + (qk * NHB + hd) * 128, 128, epi)
                        kb.dma(ACT, (qB if qk == 0 else kBd)[hd], st[:], R=[st_tk], W=[tk["qB" if qk == 0 else "kB"]])

                for hd in range(NHB):
                    g = hd // DAH
                    dil = c.pats[g][1]
                    nb = S // (128 * dil)
                    wb, wb_tk = load_w(c.oQKVB + (2 * NHB + hd) * 128, 128)
                    st, st_tk = stg.next()
                    for b4 in range(NCH // 4):
                        p, p_tk = pp.next()
                        for j in range(4):
                            blk = b4 * 4 + j
                            r, n = divmod(blk, nb)
                            t0 = n * 128 * dil + r
                            kb.mm(p[:, j * 128:(j + 1) * 128],
                                  [(xT[:, k, t0:t0 + 127 * dil + 1:dil], wb[:, k, :]) for k in range(KC)],
                                  R=[wb_tk] + xT_tk, W=[p_tk])
                        kb.op(ACT, lambda: A.copy(out=st[:, b4 * 512:(b4 + 1) * 512], in_=p[:, :]), R=[p_tk],
                              W=[st_tk])
                    kb.dma(ACT, vB[hd], st[:], R=[st_tk], W=[tk["vB"]])
                kb.barrier()

        def phase_gdn(l):
            with contextlib.ExitStack() as es:
                HG = DBG.get("hg", 1)
                NQ = NCH * DNH
                colG = sb(es, "colG", [128, NQ])
                colB = sb(es, "colB", [128, NQ])
                colNB = sb(es, "colNB", [128, NQ])
                colEG = sb(es, "colEG", [128, NQ])
                colBEG = sb(es, "colBEG", [128, NQ])
                colEGL = sb(es, "colEGL", [128, NQ])
                colEL = sb(es, "colEL", [128, NQ])
                col_tk = TK()
                nwb = sb(es, "nwb", [128, 128])
                nwb_tk = TK()
                kb.dma(SP, nwb[:], dn_norm_w[l].partition_broadcast(128), W=[nwb_tk])
                with contextlib.ExitStack() as es2:
                    ga = sb(es2, "ga", [DNH, S])
                    gb = sb(es2, "gb", [DNH, S])
                    gc = sb(es2, "gc", [DNH, S])
                    sm = sb(es2, "sm", [DNH, S])
                    al = sb(es2, "al", [DNH, 2])
                    g_tk = TK()
                    kb.dma(SP, ga[:], ab_d[0], R=[tk["ab"]], W=[g_tk])
                    kb.dma(SP, gb[:], ab_d[1], R=[tk["ab"]], W=[g_tk])
                    kb.dma(SP, sm[:], scanmask_in[:, :], W=[g_tk])
                    kb.dma(SP, al[:, 0:1], a_log[l], W=[g_tk])
                    kb.dma(SP, al[:, 1:2], dt_bias[l], W=[g_tk])
                    kb.op(ACT, lambda: A.activation(out=ga[:], in_=ga[:], func=AF.Exp, bias=al[:, 1:2], scale=1.0),
                          R=[g_tk], W=[g_tk])
                    kb.op(ACT, lambda: A.activation(out=ga[:], in_=ga[:], func=AF.Ln, bias=1.0, scale=1.0),
                          R=[g_tk], W=[g_tk])
                    kb.op(ACT, lambda: A.activation(out=al[:, 0:1], in_=al[:, 0:1], func=AF.Exp), R=[g_tk], W=[g_tk])
                    kb.op(DVE, lambda: V.tensor_scalar(out=ga[:], in0=ga[:], scalar1=al[:, 0:1], scalar2=-1.0,
                                                       op0=ALU.mult, op1=ALU.mult), R=[g_tk], W=[g_tk])
                    kb.op(DVE, lambda: V.tensor_tensor_scan(out=gc[:], data0=sm[:], data1=ga[:], initial=0.0,
                                                            op0=ALU.mult, op1=ALU.add), R=[g_tk], W=[g_tk])
                    kb.op(ACT, lambda: A.activation(out=gb[:], in_=gb[:], func=AF.Sigmoid), R=[g_tk], W=[g_tk])
                    pg = ps(es2, "pg", [128, 512])[:, 0:NQ]
                    pb = ps(es2, "pb", [128, 512])[:, 0:NQ]
                    pl = ps(es2, "pl", [128, 512])[:, 0:NQ]
                    pg_tk = TK()
                    for n in range(NCH):
                        kb.op(PE, lambda n=n: T.transpose(pg[:, n * DNH:(n + 1) * DNH], gc[:, n * 128:(n + 1) * 128],
                                                          ident_f[0:DNH, 0:DNH]), R=[g_tk, cst_tk], W=[pg_tk])
                        kb.op(PE, lambda n=n: T.transpose(pb[:, n * DNH:(n + 1) * DNH], gb[:, n * 128:(n + 1) * 128],
                                                          ident_f[0:DNH, 0:DNH]), R=[g_tk, cst_tk], W=[pg_tk])
                    kb.op(ACT, lambda: A.copy(out=colG[:], in_=pg), R=[pg_tk], W=[col_tk])
                    kb.op(ACT, lambda: A.copy(out=colB[:], in_=pb), R=[pg_tk], W=[col_tk])
                    kb.mm(pl, [(sel127, colG[:])], R=[col_tk, cst_tk], W=[pg_tk])
                    kb.op(ACT, lambda: A.activation(out=colEL[:], in_=pl, func=AF.Exp), R=[pg_tk], W=[col_tk])
                    kb.op(DVE, lambda: V.tensor_tensor(out=colEGL[:], in0=pl, in1=colG[:], op=ALU.subtract),
                          R=[pg_tk, col_tk], W=[col_tk])
                    kb.op(ACT, lambda: A.activation(out=colEGL[:], in_=colEGL[:], func=AF.Exp), R=[col_tk], W=[col_tk])
                    kb.op(ACT, lambda: A.activation(out=colEG[:], in_=colG[:], func=AF.Exp), R=[col_tk], W=[col_tk])
                    kb.op(DVE, lambda: V.tensor_tensor(out=colBEG[:], in0=colEG[:], in1=colB[:], op=ALU.mult),
                          R=[col_tk], W=[col_tk])
                    kb.op(DVE, lambda: V.tensor_scalar(out=colNB[:], in0=colB[:], scalar1=-1.0, scalar2=None,
                                                       op0=ALU.mult), R=[col_tk], W=[col_tk])
                    kb.barrier()

                if DBG.get("gdn_stage", 99) == 0:
                    return
                hb = []
                for i in range(HG):
                    d = {}
                    for nm in ("q", "k", "v", "z", "y"):
                        d[nm] = sb(es, "h%s%d" % (nm, i), [128, S], BF16)
                        d[nm + "_tk"] = TK()
                    d["S"] = sb(es, "hS%d" % i, [128, 128])
                    d["Sb"] = sb(es, "hSb%d" % i, [128, 128], BF16)
                    d["S_tk"] = TK()
                    d["tmp"] = []
                    for par in range(DBG.get("npar", 2)):
                        t = {}
                        for nm in ("KBG", "K2", "BV", "ATd", "WT", "TT", "vn", "on"):
                            t[nm] = sb(es, "t%s%d_%d" % (nm, i, par), [128, 128], BF16)
                            t[nm + "_tk"] = TK()
                        for nm in ("dg", "t1", "Ds", "t2", "DT", "Ya", "YaT", "Yb", "YbT", "P", "U", "av", "o", "sq"):
                            t[nm] = sb(es, "t%s%d_%d" % (nm, i, par), [128, 128])
                            t[nm + "_tk"] = TK()
                        t["ss"] = sb(es, "tss%d_%d" % (i, par), [128, 2])
                        t["ss_tk"] = TK()
                        d["tmp"].append(t)
                    hb.append(d)
                for i in range(HG):
                    hb[i]["B"] = (ps(es, "gB%d" % i, [128, 1024], BF16), TK())
                    for j in range(3):
                        hb[i]["F%d" % j] = (ps(es, "gF%d_%d" % (j, i), [128, 512]), TK())

                def chunk_gen(hd, d, n):
                    t = d["tmp"][n % DBG.get("npar", 2)]
                    cs = slice(n * 128, (n + 1) * 128)
                    ci = n * DNH + hd
                    cG, cB, cNB = colG[:, ci:ci + 1], colB[:, ci:ci + 1], colNB[:, ci:ci + 1]
                    cEG, cBEG, cEGL, cEL = colEG[:, ci:ci + 1], colBEG[:, ci:ci + 1], colEGL[:, ci:ci + 1], \
                        colEL[:, ci:ci + 1]
                    kT, qT, vT = d["k"][:, cs], d["q"][:, cs], d["v"][:, cs]
                    p1, p1_tk = d["B"][0][:, 0:128], d["B"][1]
                    kb.op(PE, lambda: T.transpose(p1, kT, ident_b), R=[d["k_tk"], cstb_tk], W=[p1_tk])
                    p2, p2_tk = d["B"][0][:, 128:256], d["B"][1]
                    kb.op(PE, lambda: T.transpose(p2, vT, ident_b), R=[d["v_tk"], cstb_tk], W=[p2_tk])
                    p3, p3_tk = d["F0"][0][:, 0:128], d["F0"][1]
                    kb.mm(p3, [(kT, kT)], R=[d["k_tk"]], W=[p3_tk])
                    kb.op(DVE, lambda: V.tensor_scalar(out=t["dg"][:], in0=ident_f, scalar1=cG, scalar2=None,
                                                       op0=ALU.mult), R=[cst_tk, col_tk], W=[t["dg_tk"]])
                    yield
                    kb.op(ACT, lambda: A.activation(out=t["KBG"][:], in_=p1, func=AF.Copy, scale=cBEG),
                          R=[p1_tk, col_tk], W=[t["KBG_tk"]])
                    kb.op(DVE, lambda: V.tensor_scalar(out=t["K2"][:], in0=p1, scalar1=cEGL, scalar2=None,
                                                       op0=ALU.mult), R=[p1_tk, col_tk], W=[t["K2_tk"]])
                    kb.op(ACT, lambda: A.activation(out=t["BV"][:], in_=p2, func=AF.Copy, scale=cB),
                          R=[p2_tk, col_tk], W=[t["BV_tk"]])
                    p4, p4_tk = d["F1"][0][:, 0:128], d["F1"][1]
                    kb.mm(p4, [(ones_f, t["dg"][:])], R=[t["dg_tk"], cst_tk], W=[p4_tk])
                    yield
                    kb.op(DVE, lambda: V.scalar_tensor_tensor(out=t["t1"][:], in0=p4, scalar=-1.0, in1=maskS,
                                                              op0=ALU.mult, op1=ALU.add), R=[p4_tk, cst_tk],
                          W=[t["t1_tk"]])
                    kb.op(DVE, lambda: V.scalar_tensor_tensor(out=t["t2"][:], in0=p4, scalar=cG, in1=maskT,
                                                              op0=ALU.subtract, op1=ALU.add),
                          R=[p4_tk, cst_tk, col_tk], W=[t["t2_tk"]])
                    yield
                    kb.op(ACT, lambda: A.activation(out=t["Ds"][:], in_=t["t1"][:], func=AF.Exp, bias=cG, scale=1.0),
                          R=[t["t1_tk"], col_tk], W=[t["Ds_tk"]])
                    kb.op(ACT, lambda: A.activation(out=t["DT"][:], in_=t["t2"][:], func=AF.Exp), R=[t["t2_tk"]],
                          W=[t["DT_tk"]])
                    yield
                    kb.op(DVE, lambda: V.scalar_tensor_tensor(out=t["YaT"][:], in0=p3, scalar=cNB, in1=t["Ds"][:],
                                                              op0=ALU.mult, op1=ALU.mult),
                          R=[p3_tk, col_tk, t["Ds_tk"]], W=[t["YaT_tk"]])
                    p5, p5_tk = d["F1"][0][:, 0:128], d["F1"][1]
                    kb.op(PE, lambda: T.transpose(p5, t["YaT"][:], ident_f), R=[t["YaT_tk"], cst_tk], W=[p5_tk])
                    p6, p6_tk = d["F2"][0][:, 0:128], d["F2"][1]
                    kb.mm(p6, [(kT, qT)], R=[d["k_tk"], d["q_tk"]], W=[p6_tk])
                    yield
                    sk = DBG.get("skip", [])
                    if 0 not in sk:
                        kb.op(ACT, lambda: A.copy(out=t["Ya"][:], in_=p5), R=[p5_tk], W=[t["Ya_tk"]])
                    if 1 not in sk:
                        kb.op(POOL, lambda: G.tensor_tensor(out=t["P"][:], in0=t["Ya"][:], in1=ident_f, op=ALU.add),
                              R=[t["Ya_tk"], cst_tk], W=[t["P_tk"]])
                    if 2 not in sk:
                        kb.op(DVE, lambda: V.tensor_tensor(out=t["ATd"][:], in0=p6, in1=t["DT"][:], op=ALU.mult),
                              R=[p6_tk, t["DT_tk"]], W=[t["ATd_tk"]])
                    if hd == 0 and n in DBG.get("dump6", []):
                        for nm in DBG.get("dump6_names", []):
                            dump("%s_%d" % (nm, n), t[nm][:], t[nm + "_tk"])
                        dump("colG", colG[:, 0:NQ if NQ < 128 else 128], col_tk, 128, min(NQ, 128))
                        dump("colB", colB[:, 0:NQ if NQ < 128 else 128], col_tk, 128, min(NQ, 128))
                        dump("colEGL", colEGL[:, 0:NQ if NQ < 128 else 128], col_tk, 128, min(NQ, 128))
                        dump("colEL", colEL[:, 0:NQ if NQ < 128 else 128], col_tk, 128, min(NQ, 128))
                    yield
                    cur = ("Ya", "YaT")
                    nxt = ("Yb", "YbT")
                    for it in range(6):
                        Y, YT = t[cur[0]], t[cur[1]]
                        Y_tk, YT_tk = t[cur[0] + "_tk"], t[cur[1] + "_tk"]
                        N_, NT = t[nxt[0]], t[nxt[1]]
                        N_tk, NT_tk = t[nxt[0] + "_tk"], t[nxt[1] + "_tk"]
                        last = it == 5
                        pa = None
                        if not last:
                            pa, pa_tk = d["F0"][0][:, 0:128], d["F0"][1]
                            kb.mm(pa, [(YT[:], Y[:])], R=[Y_tk, YT_tk], W=[pa_tk])
                        pbt, pbt_tk = d["F1"][0][:, 0:128], d["F1"][1]
                        kb.mm(pbt, [(Y[:], YT[:])], R=[Y_tk, YT_tk], W=[pbt_tk])
                        yield
                        if not last:
                            kb.op(ACT, lambda: A.copy(out=N_[:], in_=pa), R=[pa_tk], W=[N_tk])
                        kb.op(DVE, lambda: V.tensor_copy(out=NT[:], in_=pbt), R=[pbt_tk], W=[NT_tk])
                        pc, pc_tk = d["F2"][0][:, 0:128], d["F2"][1]
                        kb.mm(pc, [(NT[:], t["P"][:])], R=[NT_tk, t["P_tk"]], W=[pc_tk])
                        yield
                        if last:
                            kb.op(DVE, lambda: V.tensor_tensor(out=t["TT"][:], in0=pc, in1=t["P"][:], op=ALU.add),
                                  R=[pc_tk, t["P_tk"]], W=[t["TT_tk"]])
                        else:
                            kb.op(DVE, lambda: V.tensor_tensor(out=t["P"][:], in0=pc, in1=t["P"][:], op=ALU.add),
                                  R=[pc_tk, t["P_tk"]], W=[t["P_tk"]])
                        cur, nxt = nxt, cur
                        yield
                    p7, p7_tk = d["F0"][0][:, 0:128], d["F0"][1]
                    kb.mm(p7, [(t["KBG"][:], t["TT"][:])], R=[t["KBG_tk"], t["TT_tk"]], W=[p7_tk])
                    p8, p8_tk = d["F1"][0][:, 0:128], d["F1"][1]
                    kb.mm(p8, [(t["TT"][:], t["BV"][:])], R=[t["TT_tk"], t["BV_tk"]], W=[p8_tk])
                    yield
                    kb.op(ACT, lambda: A.copy(out=t["WT"][:], in_=p7), R=[p7_tk], W=[t["WT_tk"]])
                    kb.op(ACT, lambda: A.copy(out=t["U"][:], in_=p8), R=[p8_tk], W=[t["U_tk"]])
                    yield
                    p9, p9_tk = d["F2"][0][:, 0:128], d["F2"][1]
                    kb.mm(p9, [(t["WT"][:], d["Sb"][:])], R=[t["WT_tk"], d["S_tk"]], W=[p9_tk])
                    p10, p10_tk = d["F0"][0][:, 0:128], d["F0"][1]
                    kb.mm(p10, [(qT, d["Sb"][:])], R=[d["q_tk"], d["S_tk"]], W=[p10_tk])
                    yield
                    kb.op(DVE, lambda: V.tensor_tensor(out=t["vn"][:], in0=t["U"][:], in1=p9, op=ALU.subtract),
                          R=[t["U_tk"], p9_tk], W=[t["vn_tk"]])
                    yield
                    p11, p11_tk = d["F1"][0][:, 0:128], d["F1"][1]
                    kb.mm(p11, [(t["ATd"][:], t["vn"][:])], R=[t["ATd_tk"], t["vn_tk"]], W=[p11_tk])
                    p12, p12_tk = d["F2"][0][:, 0:128], d["F2"][1]
                    kb.mm(p12, [(t["K2"][:], t["vn"][:])], R=[t["K2_tk"], t["vn_tk"]], W=[p12_tk])
                    yield
                    kb.op(ACT, lambda: A.copy(out=t["av"][:], in_=p11), R=[p11_tk], W=[t["av_tk"]])
                    kb.op(DVE, lambda: V.scalar_tensor_tensor(out=d["S"][:], in0=d["S"][:], scalar=cEL, in1=p12,
                                                              op0=ALU.mult, op1=ALU.add),
                          R=[d["S_tk"], col_tk, p12_tk], W=[d["S_tk"]])
                    kb.op(ACT, lambda: A.copy(out=d["Sb"][:], in_=d["S"][:]), R=[d["S_tk"]], W=[d["S_tk"]])
                    yield
                    kb.op(DVE, lambda: V.scalar_tensor_tensor(out=t["o"][:], in0=p10, scalar=cEG, in1=t["av"][:],
                                                              op0=ALU.mult, op1=ALU.add),
                          R=[p10_tk, col_tk, t["av_tk"]], W=[t["o_tk"]])
                    kb.op(ACT, lambda: A.activation(out=t["sq"][:], in_=t["o"][:], func=AF.Square,
                                                    accum_out=t["ss"][:, 0:1]), R=[t["o_tk"]],
                          W=[t["sq_tk"], t["ss_tk"]])
                    kb.op(ACT, lambda: A.activation(out=t["ss"][:, 1:2], in_=t["ss"][:, 0:1], func=AF.Ln, bias=1e-6,
                                                    scale=1.0 / 128.0), R=[t["ss_tk"]], W=[t["ss_tk"]])
                    kb.op(ACT, lambda: A.activation(out=t["ss"][:, 1:2], in_=t["ss"][:, 1:2], func=AF.Exp, scale=-0.5),
                          R=[t["ss_tk"]], W=[t["ss_tk"]])
                    yield
                    kb.op(DVE, lambda: V.scalar_tensor_tensor(out=t["on"][:], in0=t["o"][:], scalar=t["ss"][:, 1:2],
                                                              in1=nwb[:], op0=ALU.mult, op1=ALU.mult),
                          R=[t["o_tk"], t["ss_tk"], nwb_tk], W=[t["on_tk"]])
                    p13, p13_tk = d["B"][0][:, 0:128], d["B"][1]
                    kb.op(PE, lambda: T.transpose(p13, t["on"][:], ident_b), R=[t["on_tk"], cstb_tk], W=[p13_tk])
                    yield
                    kb.op(DVE, lambda: V.tensor_tensor(out=d["y"][:, cs], in0=p13, in1=d["z"][:, cs], op=ALU.mult),
                          R=[p13_tk, d["z_tk"]], W=[d["y_tk"]])
                    if hd == 0 and n in DBG.get("dump_chunks", []):
                        for nm in ("KBG", "K2", "BV", "Ds", "DT", "TT", "ATd", "WT", "U", "vn", "av", "o", "on"):
                            dump("%s_%d" % (nm, n), t[nm][:], t[nm + "_tk"])
                        dump("S_%d" % n, d["S"][:], d["S_tk"])
                        dump("y_%d" % n, d["y"][:, cs], d["y_tk"])
                        dump("q_%d" % n, qT, d["q_tk"])
                        dump("k_%d" % n, kT, d["k_tk"])
                    yield

                for h0 in range(0, DNH, HG):
                    for i in range(HG):
                        hd = h0 + i
                        d = hb[i]
                        kb.dma(SP, d["q"][:], qkvA[hd], R=[tk["qkvA"]], W=[d["q_tk"]])
                        kb.dma(SP, d["k"][:], qkvA[DNH + hd], R=[tk["qkvA"]], W=[d["k_tk"]])
                        kb.dma(SP, d["v"][:], qkvA[2 * DNH + hd], R=[tk["qkvA"]], W=[d["v_tk"]])
                        kb.dma(SP, d["z"][:], zT[hd], R=[tk["zT"]], W=[d["z_tk"]])
                        kb.op(DVE, lambda d=d: V.memset(d["S"][:], 0.0), W=[d["S_tk"]])
                        kb.op(DVE, lambda d=d: V.memset(d["Sb"][:], 0.0), W=[d["S_tk"]])
                    for n in range(min(NCH, DBG.get("gdn_chunks", NCH))):
                        gens = [itertools.islice(chunk_gen(h0 + i, hb[i], n), DBG.get("gdn_stage", 99))
                                for i in range(HG)]
                        for _ in itertools.zip_longest(*gens):
                            pass
                    for i in range(HG):
                        kb.dma(ACT, yAd[h0 + i], hb[i]["y"][:], R=[hb[i]["y_tk"]], W=[tk["yA"]])
                kb.barrier()

        def phase_attn(l):
            with contextlib.ExitStack() as es:
                am_f = sb(es, "am_f", [128, 256])
                am = sb(es, "am", [128, 256], BF16)
                am_tk = TK()
                kb.dma(SP, am_f[:], amask_in[:, :], W=[am_tk])
                kb.op(DVE, lambda: V.tensor_copy(out=am[:], in_=am_f[:]), R=[am_tk], W=[am_tk])
                qp = Pool([sb(es, "aq%d" % i, [128, S], BF16) for i in range(2)])
                kp = Pool([sb(es, "ak%d" % i, [128, S], BF16) for i in range(2)])
                vp = Pool([sb(es, "av%d" % i, [128, S], BF16) for i in range(2)])
                accp = Pool([sb(es, "aacc%d" % i, [128, 2, S]) for i in range(2)])
                ptp = Pool([sb(es, "apt%d" % i, [128, 256], BF16) for i in range(3)])
                yst = Pool([sb(es, "ayst%d" % i, [128, S], BF16) for i in range(2)])
                pss = Pool([ps(es, "aps%d" % i, [128, 512])[:, 0:256] for i in range(4)])
                pso = Pool([ps(es, "apo%d" % i, [128, 512])[:, 0:256].rearrange("p (a b) -> p a b", a=2) for i in range(4)])
                esc = 128.0 ** -0.5
                for slot in range(DAH):
                    ac, ac_tk = accp.next()
                    for g in range(NG):
                        hd = g * DAH + slot
                        dil = c.pats[g][1]
                        nb = S // (128 * dil)
                        q_, q_tk = qp.next()
                        k_, k_tk = kp.next()
                        v_, v_tk = vp.next()
                        kb.dma(SP, q_[:], qB[hd], R=[tk["qB"]], W=[q_tk])
                        kb.dma(SP, k_[:], kBd[hd], R=[tk["kB"]], W=[k_tk])
                        kb.dma(SP, v_[:], vB[hd], R=[tk["vB"]], W=[v_tk])
                        for blk in range(NCH):
                            r, n = divmod(blk, nb)
                            cq = slice(blk * 128, (blk + 1) * 128)
                            cp = slice((blk - 1) * 128, blk * 128)
                            s_, s_tk = pss.next()
                            pt, pt_tk = ptp.next()
                            o_, o_tk = pso.next()
                            if n > 0:
                                kb.mm(s_[:, 0:128], [(k_[:, cp], q_[:, cq])], R=[k_tk, q_tk], W=[s_tk])
                                kb.mm(s_[:, 128:256], [(k_[:, cq], q_[:, cq])], R=[k_tk, q_tk], W=[s_tk])
                                lo = 0
                            else:
                                kb.mm(s_[:, 128:256], [(k_[:, cq], q_[:, cq])], R=[k_tk, q_tk], W=[s_tk])
                                lo = 128
                            kb.op(ACT, lambda: A.activation(out=pt[:, lo:256], in_=s_[:, lo:256], func=AF.Exp,
                                                            scale=esc), R=[s_tk], W=[pt_tk])
                            kb.op(POOL, lambda: G.tensor_tensor(out=pt[:, lo:256], in0=pt[:, lo:256],
                                                                in1=am[:, lo:256], op=ALU.mult), R=[pt_tk, am_tk],
                                  W=[pt_tk])
                            if n > 0:
                                kb.mm(o_[:, 0, :], [(v_[:, cp], pt[:, 0:128]), (v_[:, cq], pt[:, 128:256])],
                                      R=[v_tk, pt_tk], W=[o_tk])
                                kb.mm(o_[:, 1, :], [(ones_b, pt[:, 0:128]), (ones_b, pt[:, 128:256])],
                                      R=[pt_tk, cstb_tk], W=[o_tk])
                            else:
                                kb.mm(o_[:, 0, :], [(v_[:, cq], pt[:, 128:256])], R=[v_tk, pt_tk], W=[o_tk])
                                kb.mm(o_[:, 1, :], [(ones_b, pt[:, 128:256])], R=[pt_tk, cstb_tk], W=[o_tk])
                            t0 = n * 128 * dil + r
                            dst = ac[:, :, t0:t0 + 127 * dil + 1:dil]
                            if g == 0:
                                kb.op(DVE, lambda: V.tensor_copy(out=dst, in_=o_[:, :, :]), R=[o_tk], W=[ac_tk])
                            else:
                                kb.op(DVE, lambda: V.tensor_tensor(out=dst, in0=o_[:, :, :], in1=dst, op=ALU.add),
                                      R=[o_tk, ac_tk], W=[ac_tk])
                    ys, ys_tk = yst.next()
                    kb.op(DVE, lambda: V.reciprocal(out=ac[:, 1, :], in_=ac[:, 1, :]), R=[ac_tk], W=[ac_tk])
                    kb.op(POOL, lambda: G.tensor_tensor(out=ys[:], in0=ac[:, 0, :], in1=ac[:, 1, :], op=ALU.mult),
                          R=[ac_tk], W=[ys_tk])
                    kb.dma(ACT, yBd[slot], ys[:], R=[ys_tk], W=[tk["yB"]])
                kb.barrier()

        def load_ln(es, g_ap, b_ap, nm):
            g = sb(es, nm + "g", [128, D])
            b = sb(es, nm + "b", [128, D])
            t_ = TK()
            kb.dma(SP, g[:], g_ap.partition_broadcast(128), W=[t_])
            kb.dma(SP, b[:], b_ap.partition_broadcast(128), W=[t_])
            return g, b, t_

        def layer_norm(es_tmp, h, h_tk, g, b, gb_tk, stats, mv, st_tk, out, out_tk):
            nchunk = D // 512 if D >= 512 else 1
            w = D // nchunk
            for i in range(nchunk):
                kb.op(DVE, lambda i=i: V.bn_stats(out=stats[:, i * 6:(i + 1) * 6], in_=h[:, i * w:(i + 1) * w]),
                      R=[h_tk], W=[st_tk])
            kb.op(DVE, lambda: V.bn_aggr(out=mv[:, 0:2], in_=stats[:, 0:nchunk * 6]), R=[st_tk], W=[st_tk])
            kb.op(ACT, lambda: A.activation(out=mv[:, 2:3], in_=mv[:, 1:2], func=AF.Ln, bias=1e-5, scale=1.0),
                  R=[st_tk], W=[st_tk])
            kb.op(ACT, lambda: A.activation(out=mv[:, 2:3], in_=mv[:, 2:3], func=AF.Exp, scale=-0.5), R=[st_tk],
                  W=[st_tk])
            kb.op(DVE, lambda: V.tensor_scalar(out=h[:], in0=h[:], scalar1=mv[:, 0:1], scalar2=mv[:, 2:3],
                                               op0=ALU.subtract, op1=ALU.mult), R=[h_tk, st_tk], W=[h_tk])
            kb.op(POOL, lambda: G.tensor_tensor(out=h[:], in0=h[:], in1=g[:], op=ALU.mult), R=[h_tk, gb_tk], W=[h_tk])
            kb.op(POOL, lambda: G.tensor_tensor(out=out[:], in0=h[:], in1=b[:], op=ALU.add), R=[h_tk, gb_tk],
                  W=[out_tk] if out_tk is not h_tk else [h_tk])

        def load_weight_resident(w, es_tmp, name, src, nk, ncols):
            w_tk = TK()
            stp = Pool([sb(es_tmp, name + "_st%d" % i, [128, ncols]) for i in range(2)])
            for k in range(nk):
                s_, s_tk = stp.next()
                kb.dma(SP, s_[:], src[k * 128:(k + 1) * 128, :], W=[s_tk])
                kb.op(POOL, lambda k=k, s_=s_: G.tensor_copy(out=w[:, k, :], in_=s_[:]), R=[s_tk], W=[w_tk])
            return w, w_tk

        def phase_merge(l, src, src_tk, dst, dst_tk):
            KA, KBn = c.DNW // 128, c.DAW // 128
            with contextlib.ExitStack() as es:
                wa = sb(es, "wa", [128, KA, D], BF16)
                wbb = sb(es, "wbb", [128, KBn, D], BF16)
                with contextlib.ExitStack() as es2:
                    wa, wa_tk = load_weight_resident(wa, es2, "wa", w_ba[l], KA, D)
                    wbb, wbb_tk = load_weight_resident(wbb, es2, "wbb", w_bb[l], KBn, D)
                    kb.barrier()
                yap = Pool([sb(es, "m_ya%d" % i, [128, KA, 512], BF16) for i in range(1)])
                ybp = Pool([sb(es, "m_yb%d" % i, [128, KBn, 512], BF16) for i in range(1)])
                gap = Pool([sb(es, "m_ga%d" % i, [128, KC, 512], BF16) for i in range(1)])
                gbp = Pool([sb(es, "m_gb%d" % i, [128, KC, 512], BF16) for i in range(1)])
                mop = Pool([sb(es, "m_mo%d" % i, [128, KC, 512], BF16) for i in range(2)])
                tmp = Pool([sb(es, "m_t%d" % i, [128, 512]) for i in range(2)])
                ppa = Pool([ps(es, "m_pa%d" % i, [128, 512]) for i in range(3)])
                ppb = Pool([ps(es, "m_pb%d" % i, [128, 512]) for i in range(3)])
                for tt in range(NT5):
                    ts_ = slice(tt * 512, (tt + 1) * 512)
                    ya, ya_tk = yap.next()
                    yb, yb_tk = ybp.next()
                    ga, ga_tk = gap.next()
                    gb_, gb_tk = gbp.next()
                    mo, mo_tk = mop.next()
                    kb.dma(SP, ya[:], yAd[:, :, ts_].rearrange("k p t -> p k t"), R=[tk["yA"]], W=[ya_tk])
                    kb.dma(SP, yb[:], yBd[:, :, ts_].rearrange("k p t -> p k t"), R=[tk["yB"]], W=[yb_tk])
                    kb.dma(SP, ga[:], gAd[:, :, ts_].rearrange("k p t -> p k t"), R=[tk["gA"]], W=[ga_tk])
                    kb.dma(SP, gb_[:], gBd[:, :, ts_].rearrange("k p t -> p k t"), R=[tk["gB"]], W=[gb_tk])
                    for u in range(KC):
                        us = slice(u * 128, (u + 1) * 128)
                        pa, pa_tk = ppa.next()
                        kb.mm(pa[:, :], [(wa[:, k, us], ya[:, k, :]) for k in range(KA)], R=[wa_tk, ya_tk], W=[pa_tk])
                        pb_, pb_tk = ppb.next()
                        kb.mm(pb_[:, :], [(wbb[:, k, us], yb[:, k, :]) for k in range(KBn)], R=[wbb_tk, yb_tk],
                              W=[pb_tk])
                        t_, t_tk = tmp.next()
                        kb.op(DVE, lambda: V.tensor_tensor(out=t_[:], in0=pa[:, :], in1=ga[:, u, :], op=ALU.mult),
                              R=[pa_tk, ga_tk], W=[t_tk])
                        kb.op(DVE, lambda: V.tensor_tensor(out=mo[:, u, :], in0=pb_[:, :], in1=gb_[:, u, :],
                                                           op=ALU.mult), R=[pb_tk, gb_tk], W=[mo_tk])
                        kb.op(POOL, lambda: G.tensor_tensor(out=mo[:, u, :], in0=mo[:, u, :], in1=t_[:], op=ALU.add),
                              R=[t_tk, mo_tk], W=[mo_tk])
                    kb.dma(ACT, mrg[:, :, ts_].rearrange("k p t -> p k t"), mo[:], R=[mo_tk], W=[tk["mrg"]])
                kb.barrier()
            with contextlib.ExitStack() as es:
                wo = sb(es, "wo", [128, KC, D], BF16)
                with contextlib.ExitStack() as es2:
                    wo, wo_tk = load_weight_resident(wo, es2, "wo", w_out[l], KC, D)
                    kb.barrier()
                g, b, gb_tk = load_ln(es, ln1_g[l], ln1_b[l], "ln1")
                mip = Pool([sb(es, "o_mi%d" % i, [128, KC, 128], BF16) for i in range(2)])
                xp = Pool([sb(es, "o_x%d" % i, [128, D]) for i in range(2)])
                hp = Pool([sb(es, "o_h%d" % i, [128, D]) for i in range(2)])
                stats = sb(es, "o_stats", [128, 6 * 8])
                mv = sb(es, "o_mv", [128, 4])
                st_tk = TK()
                NB = (D + 511) // 512
                pm = Pool([ps(es, "o_pm%d" % i, [128, 512]) for i in range(min(8, 2 * NB))])
                for tt in range(S // 128):
                    ts_ = slice(tt * 128, (tt + 1) * 128)
                    mi, mi_tk = mip.next()
                    x_, x_tk = xp.next()
                    h, h_tk = hp.next()
                    kb.dma(SP, mi[:], mrg[:, :, ts_].rearrange("k p t -> p k t"), R=[tk["mrg"]], W=[mi_tk])
                    kb.dma(SP, x_[:], src[ts_, :], R=[src_tk], W=[x_tk])
                    for cb in range(NB):
                        w = min(512, D - cb * 512)
                        cs = slice(cb * 512, cb * 512 + w)
                        p, p_tk = pm.next()
                        kb.mm(p[:, 0:w], [(mi[:, k, :], wo[:, k, cs]) for k in range(KC)], R=[mi_tk, wo_tk], W=[p_tk])
                        kb.op(DVE, lambda: V.scalar_tensor_tensor(out=h[:, cs], in0=x_[:, cs], scalar=float(c.alpha),
                                                                  in1=p[:, 0:w], op0=ALU.mult, op1=ALU.add),
                              R=[x_tk, p_tk], W=[h_tk])
                    layer_norm(es, h, h_tk, g, b, gb_tk, stats, mv, st_tk, h, h_tk)
                    kb.dma(ACT, dst[ts_, :], h[:], R=[h_tk], W=[dst_tk])
                kb.barrier()

        def phase_moe(l, src, src_tk, dst, dst_tk):
            with contextlib.ExitStack() as es:
                xT, xT_tk = phase_transpose(es, src, src_tk)
                combT = sb(es, "combT", [NE, S])
                combT_tk = TK()
                with contextlib.ExitStack() as es2:
                    wr_f = sb(es2, "wr_f", [128, KC, NE])
                    wr = sb(es2, "wr", [128, KC, NE], BF16)
                    rb = sb(es2, "rb", [128, NE])
                    wr_tk = TK()
                    kb.dma(SP, wr_f[:], router_w[l].rearrange("(k p) e -> p k e", p=128), W=[wr_tk])
                    kb.dma(SP, rb[:], router_b[l].partition_broadcast(128), W=[wr_tk])
                    kb.op(DVE, lambda: V.tensor_copy(out=wr[:], in_=wr_f[:]), R=[wr_tk], W=[wr_tk])
                    lg = Pool([sb(es2, "lg%d" % i, [128, NE]) for i in range(2)])
                    ex = Pool([sb(es2, "ex%d" % i, [128, NE]) for i in range(2)])
                    t8 = Pool([sb(es2, "t8%d" % i, [128, 16]) for i in range(2)])
                    prr = Pool([ps(es2, "prr%d" % i, [128, 512])[:, 0:NE] for i in range(2)])
                    prt = Pool([ps(es2, "prt%d" % i, [128, 512])[0:NE, 0:128] for i in range(2)])
                    for tt in range(S // 128):
                        ts_ = slice(tt * 128, (tt + 1) * 128)
                        p, p_tk = prr.next()
                        kb.mm(p[:, :], [(xT[:, k, ts_], wr[:, k, :]) for k in range(KC)], R=[wr_tk, xT_tk[tt // 4]],
                              W=[p_tk])
                        lgt, lg_tk = lg.next()
                        kb.op(DVE, lambda: V.tensor_tensor(out=lgt[:], in0=p[:, :], in1=rb[:], op=ALU.add),
                              R=[p_tk, wr_tk], W=[lg_tk])
                        t_, t_tk = t8.next()
                        kb.op(DVE, lambda: V.max(out=t_[:, 0:8], in_=lgt[:]), R=[lg_tk], W=[t_tk])
                        kb.op(DVE, lambda: V.tensor_scalar(out=t_[:, 8:9], in0=t_[:, 0:1], scalar1=-1.0, scalar2=None,
                                                           op0=ALU.mult), R=[t_tk], W=[t_tk])
                        e_, e_tk = ex.next()
                        kb.op(ACT, lambda: A.activation(out=e_[:], in_=lgt[:], func=AF.Exp, bias=t_[:, 8:9],
                                                        scale=1.0), R=[lg_tk, t_tk], W=[e_tk])
                        kb.op(DVE, lambda: V.tensor_scalar(out=lgt[:], in0=lgt[:], scalar1=t_[:, 3:4], scalar2=None,
                                                           op0=ALU.is_ge), R=[lg_tk, t_tk], W=[lg_tk])
                        kb.op(DVE, lambda: V.tensor_tensor(out=e_[:], in0=e_[:], in1=lgt[:], op=ALU.mult),
                              R=[lg_tk, e_tk], W=[e_tk])
                        kb.op(DVE, lambda: V.reduce_sum(out=t_[:, 9:10], in_=e_[:], axis=AX.X), R=[e_tk], W=[t_tk])
                        kb.op(DVE, lambda: V.reciprocal(out=t_[:, 10:11], in_=t_[:, 9:10]), R=[t_tk], W=[t_tk])
                        kb.op(DVE, lambda: V.tensor_scalar(out=e_[:], in0=e_[:], scalar1=t_[:, 10:11], scalar2=None,
                                                           op0=ALU.mult), R=[e_tk, t_tk], W=[e_tk])
                        pt, pt_tk = prt.next()
                        kb.op(PE, lambda: T.transpose(pt[:, :], e_[:], ident_f), R=[e_tk, cst_tk], W=[pt_tk])
                        kb.op(ACT, lambda: A.copy(out=combT[:, ts_], in_=pt[:, :]), R=[pt_tk], W=[combT_tk])
                    kb.dma(ACT, combT_d[:, :], combT[:], R=[combT_tk], W=[tk["combT"]])
                    kb.barrier()
                sele = sb(es, "sele", [NE, NE * 128])
                bgu = sb(es, "bgu", [128, NE * 2 * EC])
                se_tk = TK()
                kb.dma(SP, sele[:], sele_in[:, :], W=[se_tk])
                kb.dma(SP, bgu[:], b_gu[l], W=[se_tk])
                wst = Pool([sb(es, "e_wst%d" % i, [128, KC, 128]) for i in range(2)])
                wbf = Pool([sb(es, "e_wbf%d" % i, [128, KC, 128], BF16) for i in range(3)])
                gg = Pool([sb(es, "e_g%d" % i, [128, 512]) for i in range(2)])
                sg = Pool([sb(es, "e_s%d" % i, [128, 512]) for i in range(2)])
                uu = Pool([sb(es, "e_u%d" % i, [128, 512]) for i in range(2)])
                hst = Pool([sb(es, "e_h%d" % i, [128, 512], BF16) for i in range(2)])
                pg_ = Pool([ps(es, "e_pg%d" % i, [128, 512]) for i in range(3)])
                pu_ = Pool([ps(es, "e_pu%d" % i, [128, 512]) for i in range(3)])
                pc_ = Pool([ps(es, "e_pc%d" % i, [128, 512]) for i in range(2)])

                def load_w(e, col0):
                    ws, ws_tk = wst.next()
                    kb.dma(SP, ws[:], w_gu[l, e, :, col0:col0 + 128].rearrange("(k p) c -> p k c", p=128), W=[ws_tk])
                    wb, wb_tk = wbf.next()
                    kb.op(POOL, lambda: G.tensor_copy(out=wb[:], in_=ws[:]), R=[ws_tk], W=[wb_tk])
                    return wb, wb_tk

                for e in range(NE):
                    for j in range(EC):
                        wg, wg_tk = load_w(e, j * 128)
                        wu, wu_tk = load_w(e, DE + j * 128)
                        bg = bgu[:, e * 2 * EC + j:e * 2 * EC + j + 1]
                        bu = bgu[:, e * 2 * EC + EC + j:e * 2 * EC + EC + j + 1]
                        for tt in range(NT5):
                            ts_ = slice(tt * 512, (tt + 1) * 512)
                            hs, hs_tk = hst.next()
                            p1, p1_tk = pg_.next()
                            kb.mm(p1[:, :], [(wg[:, k, :], xT[:, k, ts_]) for k in range(KC)], R=[wg_tk, xT_tk[tt]],
                                  W=[p1_tk])
                            p2, p2_tk = pu_.next()
                            kb.mm(p2[:, :], [(wu[:, k, :], xT[:, k, ts_]) for k in range(KC)], R=[wu_tk, xT_tk[tt]],
                                  W=[p2_tk])
                            p3, p3_tk = pc_.next()
                            kb.mm(p3[:, :], [(sele[:, e * 128:(e + 1) * 128], combT[:, ts_])], R=[se_tk, combT_tk],
                                  W=[p3_tk])
                            g_, g_tk = gg.next()
                            s_, s_tk = sg.next()
                            u_, u_tk = uu.next()
                            kb.op(DVE, lambda: V.tensor_scalar(out=g_[:], in0=p1[:, :], scalar1=bg, scalar2=7.0,
                                                               op0=ALU.add, op1=ALU.min), R=[p1_tk, se_tk], W=[g_tk])
                            kb.op(ACT, lambda: A.activation(out=s_[:], in_=g_[:], func=AF.Sigmoid, scale=1.702),
                                  R=[g_tk], W=[s_tk])
                            kb.op(DVE, lambda: V.tensor_scalar(out=u_[:], in0=p2[:, :], scalar1=bu, scalar2=7.0,
                                                               op0=ALU.add, op1=ALU.min), R=[p2_tk, se_tk], W=[u_tk])
                            kb.op(POOL, lambda: G.tensor_scalar(out=u_[:], in0=u_[:], scalar1=-7.0, scalar2=1.0,
                                                                op0=ALU.max, op1=ALU.add), R=[u_tk], W=[u_tk])
                            kb.op(DVE, lambda: V.tensor_tensor(out=g_[:], in0=g_[:], in1=s_[:], op=ALU.mult),
                                  R=[g_tk, s_tk], W=[g_tk])
                            kb.op(DVE, lambda: V.tensor_tensor(out=u_[:], in0=p3[:, :], in1=u_[:], op=ALU.mult),
                                  R=[p3_tk, u_tk], W=[u_tk])
                            kb.op(DVE, lambda: V.tensor_tensor(out=hs[:], in0=g_[:], in1=u_[:], op=ALU.mult),
                                  R=[g_tk, u_tk], W=[hs_tk])
                            kb.dma(ACT, hT[e * EC + j][:, ts_], hs[:], R=[hs_tk], W=[tk["hT"]])
                kb.barrier()
            with contextlib.ExitStack() as es:
                TB = min(1024, S)
                NSB = TB // 128
                NB = (D + 511) // 512
                yacc = sb(es, "yacc", [128, NSB, D])
                yacc_tk = [TK() for _ in range(NSB)]
                g, b, gb_tk = load_ln(es, ln2_g[l], ln2_b[l], "ln2")
                bd = sb(es, "bd", [NE, D])
                bd_tk = TK()
                kb.dma(SP, bd[:], b_dn[l], W=[bd_tk])
                cT = sb(es, "cT", [NE, TB])
                cT_tk = TK()
                xp = Pool([sb(es, "d_x%d" % i, [128, D]) for i in range(2)])
                hp = Pool([sb(es, "d_h%d" % i, [128, EC, TB], BF16) for i in range(2)])
                wsp = Pool([sb(es, "d_ws%d" % i, [128, D]) for i in range(2)])
                wdp = Pool([sb(es, "d_wd%d" % i, [128, EC, D], BF16) for i in range(2)])
                stats = sb(es, "d_stats", [128, 6 * 8])
                mv = sb(es, "d_mv", [128, 4])
                st_tk = TK()
                pm = Pool([ps(es, "d_pm%d" % i, [128, 512]) for i in range(6)])
                for tb in range(S // TB):
                    t0 = tb * TB
                    kb.dma(SP, cT[:], combT_d[:, t0:t0 + TB], R=[tk["combT"]], W=[cT_tk])
                    for sbt in range(NSB):
                        x_, x_tk = xp.next()
                        kb.dma(SP, x_[:], src[t0 + sbt * 128:t0 + (sbt + 1) * 128, :], R=[src_tk], W=[x_tk])
                        for cb in range(NB):
                            w = min(512, D - cb * 512)
                            cs = slice(cb * 512, cb * 512 + w)
                            p, p_tk = pm.next()
                            kb.mm(p[:, 0:w], [(cT[:, sbt * 128:(sbt + 1) * 128], bd[:, cs])], R=[cT_tk, bd_tk],
                                  W=[p_tk])
                            kb.op(DVE, lambda: V.scalar_tensor_tensor(out=yacc[:, sbt, cs], in0=x_[:, cs],
                                                                      scalar=float(c.alpha), in1=p[:, 0:w],
                                                                      op0=ALU.mult, op1=ALU.add),
                                  R=[x_tk, p_tk], W=[yacc_tk[sbt]])
                    for e in range(NE):
                        h_, h_tk = hp.next()
                        kb.dma(SP, h_[:], hT[e * EC:(e + 1) * EC, :, t0:t0 + TB].rearrange("j p t -> p j t"),
                               R=[tk["hT"]], W=[h_tk])
                        wd, wd_tk = wdp.next()
                        for j in range(EC):
                            ws, ws_tk = wsp.next()
                            kb.dma(SP, ws[:], w_dn[l, e, j * 128:(j + 1) * 128, :], W=[ws_tk])
                            kb.op(POOL, lambda j=j, ws=ws: G.tensor_copy(out=wd[:, j, :], in_=ws[:]), R=[ws_tk],
                                  W=[wd_tk])
                        for sbt in range(NSB):
                            for cb in range(NB):
                                w = min(512, D - cb * 512)
                                cs = slice(cb * 512, cb * 512 + w)
                                p, p_tk = pm.next()
                                kb.mm(p[:, 0:w], [(h_[:, j, sbt * 128:(sbt + 1) * 128], wd[:, j, cs])
                                                  for j in range(EC)], R=[h_tk, wd_tk], W=[p_tk])
                                kb.op(DVE, lambda: V.tensor_tensor(out=yacc[:, sbt, cs], in0=p[:, 0:w],
                                                                   in1=yacc[:, sbt, cs], op=ALU.add),
                                      R=[p_tk, yacc_tk[sbt]], W=[yacc_tk[sbt]])
                    for sbt in range(NSB):
                        layer_norm(es, yacc[:, sbt, :], yacc_tk[sbt], g, b, gb_tk, stats, mv, st_tk,
                                   yacc[:, sbt, :], yacc_tk[sbt])
                        kb.dma(ACT, dst[t0 + sbt * 128:t0 + (sbt + 1) * 128, :], yacc[:, sbt, :],
                               R=[yacc_tk[sbt]], W=[dst_tk])
                kb.barrier()

        cur, cur_tk = x_in, TK()
        for l in range(L):
            on = lambda p: phases is None or p in phases
            if on("proj"):
                phase_proj(l, cur, cur_tk)
            if on("gdn"):
                phase_gdn(l)
            if on("attn"):
                phase_attn(l)
            if on("merge"):
                phase_merge(l, cur, cur_tk, xa, tk["xa"])
            last = l == L - 1
            dst, dst_tk = (y_out, tk["y"]) if last else (xb, tk["xb"])
            if on("moe"):
                phase_moe(l, xa, tk["xa"], dst, dst_tk)
            cur, cur_tk = xb, tk["xb"]
        kb.barrier()
    if ret_rec:
        return nc, kb.rec
    return nc


def build_two_pass(cfg, **kw):
    _, rec = build_program(cfg, ret_rec=True, **kw)
    return build_program(cfg, needed=rec, **kw)


def host_consts(cfg):
    c = cfg
    S = c.S
    i = np.arange(128)
    ident = np.eye(128, dtype=np.float32)
    ones = np.ones((128, 128), np.float32)
    maskS = np.where(i[:, None] > i[None, :], 0.0, NEG).astype(np.float32)
    maskT = np.where(i[:, None] <= i[None, :], 0.0, NEG).astype(np.float32)
    rm = np.zeros((128, 128), np.float32)
    for m in range(64):
        rm[m + 64, m] = -1.0
    for m in range(64, 128):
        rm[m - 64, m] = 1.0
    sel127 = np.zeros((128, 128), np.float32)
    sel127[127, :] = 1.0
    consts = np.concatenate([ident, ones, maskS, maskT, rm, sel127], axis=1)
    amask = np.concatenate([(i[:, None] >= i[None, :]), (i[:, None] <= i[None, :])], axis=1).astype(np.float32)
    half = 64
    inv_freq = (10000.0 ** (-np.arange(half, dtype=np.float32) / half)).astype(np.float32)
    ang = np.arange(S, dtype=np.float32)[None, :] * np.concatenate([inv_freq, inv_freq])[:, None]
    cos_t = np.cos(ang).astype(np.float32)
    sin_t = np.sin(ang).astype(np.float32)
    scanmask = np.ones((c.DNH, S), np.float32)
    scanmask[:, ::128] = 0.0
    sele = np.zeros((c.NE, c.NE * 128), np.float32)
    for e in range(c.NE):
        sele[e, e * 128:(e + 1) * 128] = 1.0
    return dict(consts=consts, amask=amask, cos_t=cos_t, sin_t=sin_t, scanmask=scanmask, sele=sele)


def host_layout(cfg, inp):
    c = cfg
    L = c.L
    out = {}
    f = lambda a: np.ascontiguousarray(np.asarray(a, dtype=np.float32))
    out["w_in"] = f(inp["w_in"])
    cw = np.asarray(inp["conv_w"], np.float32)
    out["conv_w"] = f(cw.transpose(0, 2, 1).reshape(L, 3 * c.DNH, 128, 4).transpose(0, 2, 1, 3))
    out["a_log"] = f(np.asarray(inp["a_log"]).reshape(L, c.DNH, 1))
    out["dt_bias"] = f(np.asarray(inp["dt_bias"]).reshape(L, c.DNH, 1))
    out["dn_norm_w"] = f(inp["dn_norm_w"])
    for k in ("w_branch_a", "w_branch_b", "w_out", "ln1_g", "ln1_b", "router_w", "router_b", "b_down", "ln2_g",
              "ln2_b"):
        out[k] = f(inp[k])
    out["w_gate_up"] = f(inp["w_gate_up"])
    out["w_down"] = f(inp["w_down"])
    bg = np.asarray(inp["b_gate_up"], np.float32)
    EC2 = 2 * c.DE // 128
    out["b_gate_up"] = f(bg.reshape(L, c.NE, EC2, 128).transpose(0, 3, 1, 2).reshape(L, 128, c.NE * EC2))
    return out


_CACHE = {}


def kernel(**inputs):
    cfg = Cfg()
    x = np.asarray(inputs["x"], np.float32)
    B = x.shape[0]
    shared = host_layout(cfg, inputs)
    shared.update(host_consts(cfg))
    if "nc" not in _CACHE:
        _CACHE["nc"] = build_two_pass(cfg)
    nc = _CACHE["nc"]
    in_maps = []
    for b in range(B):
        m = dict(shared)
        m["x"] = np.ascontiguousarray(x[b])
        in_maps.append(m)
    res = run_bass_kernel_spmd(nc, in_maps, core_ids=list(range(B)))
    return np.stack([np.asarray(r["y"], np.float32) for r in res.results], axis=0)
```

```python
import contextlib
import itertools
import math

import numpy as np
import concourse.bass as bass
import concourse.mybir as mybir
from concourse.bass_utils import run_bass_kernel_spmd

F32 = mybir.dt.float32
BF16 = mybir.dt.bfloat16
AF = mybir.ActivationFunctionType
ALU = mybir.AluOpType
AX = mybir.AxisListType

NEG = -30000.0
DBG = {}


class Cfg:
    def __init__(self, D=2048, S=4096, L=4, DNH=16, DAH=8, NE=32, DE=512, alpha=8 ** 0.25,
                 pats=((128, 1), (512, 4), (2048, 16))):
        self.D, self.S, self.L, self.DNH, self.DAH, self.NE, self.DE = D, S, L, DNH, DAH, NE, DE
        self.alpha = alpha
        self.pats = pats
        self.KC = D // 128
        self.DNW = DNH * 128
        self.DAW = DAH * 128
        self.NG = len(pats)
        self.sizes = (3 * self.DNW, self.DNW, DNH, DNH, 3 * self.NG * self.DAW, D, D)
        self.INW = sum(self.sizes)
        o = np.cumsum((0,) + self.sizes)
        self.oQKVA, self.oZ, self.oA, self.oB, self.oQKVB, self.oGA, self.oGB = [int(v) for v in o[:7]]
        self.NCH = S // 128
        self.TOPK = 4


class TK:
    __slots__ = ("w", "r")

    def __init__(self):
        self.w = None
        self.r = {}


class EngW:
    def __init__(self, kb, eng, name, is_pe=False):
        self.eng = eng
        self.name = name
        self.is_pe = is_pe
        self.sem = kb.nc.alloc_semaphore("prog_" + name)
        self.key = "E" + name
        self.cnt = 0
        self.sval = 0
        self.seen = {}
        self.dsems = []
        self.dvals = []
        self.dnext = 0


class KB:
    def __init__(self, nc, ndma=6, needed=None):
        self.nc = nc
        self.needed = needed
        self.rec = {}
        self.pe = EngW(self, nc.tensor, "pe", True)
        self.act = EngW(self, nc.scalar, "act")
        self.dve = EngW(self, nc.vector, "dve")
        self.pool = EngW(self, nc.gpsimd, "pool")
        self.sp = EngW(self, nc.sync, "sp")
        self.all = [self.pe, self.act, self.dve, self.pool, self.sp]
        for e in (self.sp, self.act, self.pool):
            for i in range(ndma):
                e.dsems.append(nc.alloc_semaphore("dma_%s_%d" % (e.name, i)))
                e.dvals.append(0)
        self.uid = 0

    def _wait(self, e, deps):
        for (sem, val, key) in deps:
            if key == e.key and (e.is_pe or DBG.get("no_self_wait", False)):
                continue
            if e.seen.get(key, 0) < val:
                e.eng.wait_ge(sem, val)
                e.seen[key] = val
                if key[0] == "E":
                    self.rec.setdefault(key, set()).add(val)

    def _tok(self, e, ins):
        e.cnt += 1
        if self.needed is None:
            ins.then_inc(e.sem, 1)
            return (e.sem, e.cnt, e.key)
        if e.cnt in self.needed.get(e.key, ()):
            ins.then_inc(e.sem, 1)
            e.sval += 1
        return (e.sem, e.sval, e.key)

    def _deps(self, R, W):
        deps = []
        for t in R:
            if t.w is not None:
                deps.append(t.w)
        for t in W:
            if t.w is not None:
                deps.append(t.w)
            deps.extend(t.r.values())
        return deps

    def _commit(self, tok, R, W):
        for t in W:
            t.w = tok
            t.r = {}
        for t in R:
            if t not in W:
                t.r[tok[2]] = tok

    def op(self, e, fn, R=(), W=()):
        self._wait(e, self._deps(R, W))
        ins = fn()
        tok = self._tok(e, ins)
        self._commit(tok, R, W)
        return tok

    def mm(self, out, pairs, R=(), W=(), fp32=False):
        e = self.pe
        self._wait(e, self._deps(R, W))
        n = len(pairs)
        ins = None
        for i, (l, r) in enumerate(pairs):
            ins = self.nc.tensor.matmul(out, l, r, start=(i == 0), stop=(i == n - 1))
        tok = self._tok(e, ins)
        self._commit(tok, R, W)
        return tok

    def dma(self, q, out, in_, R=(), W=(), nc_ok=False):
        i = q.dnext
        q.dnext = (q.dnext + 1) % len(q.dsems)
        sem = q.dsems[i]
        key = "D%s%d" % (q.name, i)
        deps = self._deps(R, W)
        if q.dvals[i] > 0:
            deps.append((sem, q.dvals[i], key))
        self._wait(q, deps)
        if nc_ok:
            with self.nc.allow_non_contiguous_dma(reason="small strided"):
                ins = q.eng.dma_start(out=out, in_=in_)
        else:
            ins = q.eng.dma_start(out=out, in_=in_)
        q.dvals[i] += 16
        ins.then_inc(sem, 16)
        tok = (sem, q.dvals[i], key)
        self._commit(tok, R, W)
        return tok

    def barrier(self):
        toks = [(e.sem, e.cnt if self.needed is None else e.sval, e.key) for e in self.all if e.cnt > 0]
        for q in (self.sp, self.act, self.pool):
            for i, s in enumerate(q.dsems):
                if q.dvals[i] > 0:
                    toks.append((s, q.dvals[i], "D%s%d" % (q.name, i)))
        for e in self.all:
            for (sem, val, key) in toks:
                if key == e.key:
                    continue
                if e.seen.get(key, 0) < val:
                    e.eng.wait_ge(sem, val)
                    e.seen[key] = val
                    if key[0] == "E":
                        self.rec.setdefault(key, set()).add(val)


class Pool:
    def __init__(self, tiles):
        self.tiles = [(t, TK()) for t in tiles]
        self.i = 0

    def next(self):
        t = self.tiles[self.i]
        self.i = (self.i + 1) % len(self.tiles)
        return t


def build_program(cfg, debug_outs=(), phases=None, needed=None, ret_rec=False):
    c = cfg
    nc = bass.Bass("TRN2", target_bir_lowering=False)
    D, S, L, KC, NCH = c.D, c.S, c.L, c.KC, c.NCH
    DNH, DAH, NE, DE, NG = c.DNH, c.DAH, c.NE, c.DE, c.NG
    NT5 = S // 512
    NHB = NG * DAH
    EC = DE // 128
    NUQ = 3 * DNH

    def din(name, shape):
        return nc.dram_tensor(name, list(shape), F32, kind="ExternalInput").ap()

    x_in = din("x", [S, D])
    w_in = din("w_in", [L, D, c.INW])
    conv_w = din("conv_w", [L, 128, NUQ, 4])
    a_log = din("a_log", [L, DNH, 1])
    dt_bias = din("dt_bias", [L, DNH, 1])
    dn_norm_w = din("dn_norm_w", [L, 128])
    w_ba = din("w_branch_a", [L, c.DNW, D])
    w_bb = din("w_branch_b", [L, c.DAW, D])
    w_out = din("w_out", [L, D, D])
    ln1_g = din("ln1_g", [L, D])
    ln1_b = din("ln1_b", [L, D])
    router_w = din("router_w", [L, D, NE])
    router_b = din("router_b", [L, NE])
    w_gu = din("w_gate_up", [L, NE, D, 2 * DE])
    b_gu = din("b_gate_up", [L, 128, NE * 2 * EC])
    w_dn = din("w_down", [L, NE, DE, D])
    b_dn = din("b_down", [L, NE, D])
    ln2_g = din("ln2_g", [L, D])
    ln2_b = din("ln2_b", [L, D])
    consts = din("consts", [128, 6 * 128])
    amask_in = din("amask", [128, 256])
    cos_in = din("cos_t", [128, S])
    sin_in = din("sin_t", [128, S])
    scanmask_in = din("scanmask", [DNH, S])
    sele_in = din("sele", [NE, NE * 128])
    y_out = nc.dram_tensor("y", [S, D], F32, kind="ExternalOutput").ap()

    dbg = {}

    def dscr(name, shape, dt):
        kind = "ExternalOutput" if name in debug_outs else "Internal"
        t = nc.dram_tensor(name, list(shape), dt, kind=kind).ap()
        dbg[name] = t
        return t

    xa = dscr("xa", [S, D], F32)
    xb = dscr("xb", [S, D], F32)
    qkvA = dscr("qkvA", [NUQ, 128, S], BF16)
    zT = dscr("zT", [DNH, 128, S], BF16)
    ab_d = dscr("ab_d", [2, DNH, S], F32)
    qB = dscr("qB", [NHB, 128, S], BF16)
    kBd = dscr("kB", [NHB, 128, S], BF16)
    vB = dscr("vB", [NHB, 128, NCH * 128], BF16)
    gAd = dscr("gA", [KC, 128, S], BF16)
    gBd = dscr("gB", [KC, 128, S], BF16)
    yAd = dscr("yA", [DNH, 128, S], BF16)
    yBd = dscr("yB", [DAH, 128, S], BF16)
    mrg = dscr("mrg", [KC, 128, S], BF16)
    hT = dscr("hT", [NE * EC, 128, S], BF16)
    combT_d = dscr("combT", [NE, S], F32)

    kb = KB(nc, needed=needed)
    PE, ACT, DVE, POOL, SP = kb.pe, kb.act, kb.dve, kb.pool, kb.sp
    NDUMP = 40
    dumpT = dscr("dumpT", [NDUMP, 128, 128], F32) if "dumpT" in debug_outs else None
    dump_names = []
    dump_tk = TK()

    def dump(name, ap, ap_tk, np_=128, nf=128):
        if dumpT is None:
            return
        i = len(dump_names)
        dump_names.append(name)
        DBG.setdefault("dump_names", []).append(name)
        st_ = dump_st[i % 2]
        kb.op(DVE, lambda: nc.vector.memset(st_[0][:], 0.0), W=[st_[1]])
        kb.op(DVE, lambda: nc.vector.tensor_scalar(out=st_[0][0:np_, 0:nf], in0=ap, scalar1=-1e30, scalar2=1e30,
                                                   op0=ALU.max, op1=ALU.min), R=[ap_tk], W=[st_[1]])
        kb.dma(SP, dumpT[i], st_[0][:], R=[st_[1]], W=[dump_tk])
    V, A, G, T = nc.vector, nc.scalar, nc.gpsimd, nc.tensor

    tk = {n: TK() for n in ("xa", "xb", "qkvA", "zT", "ab", "qB", "kB", "vB", "gA", "gB", "yA", "yB", "mrg",
                            "hT", "combT", "y")}

    with contextlib.ExitStack() as top:
        uid = [0]

        def sb(es, name, shape, dt=F32):
            uid[0] += 1
            return es.enter_context(nc.sbuf_tensor("%s_%d" % (name, uid[0]), list(shape), dt))

        def ps(es, name, shape, dt=F32):
            uid[0] += 1
            return es.enter_context(nc.psum_tensor("%s_%d" % (name, uid[0]), list(shape), dt))

        cst = sb(top, "cst", [128, 6 * 128])
        cst_tk = TK()
        kb.dma(SP, cst[:], consts[:, :], W=[cst_tk])
        ident_f = cst[:, 0:128]
        ones_f = cst[:, 128:256]
        maskS = cst[:, 256:384]
        maskT = cst[:, 384:512]
        rmat = cst[:, 512:640]
        sel127 = cst[:, 640:768]
        cstb = sb(top, "cstb", [128, 2 * 128], BF16)
        cstb_tk = TK()
        kb.op(DVE, lambda: V.tensor_copy(out=cstb[:], in_=cst[:, 0:256]), R=[cst_tk], W=[cstb_tk])
        ident_b = cstb[:, 0:128]
        ones_b = cstb[:, 128:256]
        dump_st = [(sb(top, "dump_st%d" % i, [128, 128]), TK()) for i in range(2)]
        CT = [cst_tk, cstb_tk]

        def phase_transpose(es, src, src_tk):
            xT = sb(es, "xT", [128, KC, S], BF16)
            xT_tk = [TK() for _ in range(NT5)]
            with contextlib.ExitStack() as es2:
                xin = Pool([sb(es2, "xin%d" % i, [128, D]) for i in range(2)])
                pst = Pool([ps(es2, "pst%d" % i, [128, 512]) for i in range(4)])
                flip = 0
                for tt in range(S // 128):
                    xi, xi_tk = xin.next()
                    kb.dma(SP, xi[:], src[tt * 128:(tt + 1) * 128, :], R=[src_tk], W=[xi_tk])
                    for k4 in range((KC + 3) // 4):
                        nk = min(4, KC - k4 * 4)
                        p, p_tk = pst.next()
                        for j in range(nk):
                            kk = k4 * 4 + j
                            kb.op(PE, lambda j=j, kk=kk: T.transpose(p[:, j * 128:(j + 1) * 128],
                                                                     xi[:, kk * 128:(kk + 1) * 128], ident_f),
                                  R=[xi_tk, cst_tk], W=[p_tk])
                        o = xT[:, k4 * 4:k4 * 4 + nk, tt * 128:(tt + 1) * 128]
                        i_ = p[:, 0:nk * 128].rearrange("p (j t) -> p j t", j=nk)
                        if flip:
                            kb.op(DVE, lambda: V.tensor_copy(out=o, in_=i_), R=[p_tk], W=[xT_tk[tt // 4]])
                        else:
                            kb.op(ACT, lambda: A.copy(out=o, in_=i_), R=[p_tk], W=[xT_tk[tt // 4]])
                        flip ^= 1
                kb.barrier()
            return xT, xT_tk

        def phase_proj(l, src, src_tk):
            with contextlib.ExitStack() as es:
                xT, xT_tk = phase_transpose(es, src, src_tk)
                wst = Pool([sb(es, "wst%d" % i, [128, KC, 128]) for i in range(2)])
                wbf = Pool([sb(es, "wbf%d" % i, [128, KC, 128], BF16) for i in range(2)])
                stg = Pool([sb(es, "stg%d" % i, [128, S], BF16) for i in range(2)])
                stfp = Pool([sb(es, "stf%d" % i, [DNH, 512]) for i in range(2)])
                cw = sb(es, "cw", [128, NUQ, 4])
                cw_tk = TK()
                kb.dma(SP, cw[:], conv_w[l], W=[cw_tk])
                raw = Pool([sb(es, "raw%d" % i, [128, 3 + 512]) for i in range(2)])
                acc = Pool([sb(es, "acc%d" % i, [128, 512]) for i in range(2)])
                t1p = Pool([sb(es, "t1p%d" % i, [128, 512]) for i in range(2)])
                t2p = Pool([sb(es, "t2p%d" % i, [128, 512]) for i in range(2)])
                csp = Pool([sb(es, "csp%d" % i, [128, 2, 512]) for i in range(2)])
                pp = Pool([ps(es, "pp%d" % i, [128, 512]) for i in range(5)])
                pq = Pool([ps(es, "pq%d" % i, [128, 512]) for i in range(2)])

                def load_w(col0, ncols):
                    ws, ws_tk = wst.next()
                    kb.dma(SP, ws[:, :, 0:ncols],
                           w_in[l, :, col0:col0 + ncols].rearrange("(k p) c -> p k c", p=128), W=[ws_tk])
                    wb, wb_tk = wbf.next()
                    kb.op(POOL, lambda: G.tensor_copy(out=wb[:, :, 0:ncols], in_=ws[:, :, 0:ncols]),
                          R=[ws_tk], W=[wb_tk])
                    return wb, wb_tk

                def fm_unit(col0, ncols, epi):
                    wb, wb_tk = load_w(col0, ncols)
                    for tt in range(NT5):
                        p, p_tk = pp.next()
                        kb.mm(p[0:ncols, :], [(wb[:, k, 0:ncols], xT[:, k, tt * 512:(tt + 1) * 512])
                                              for k in range(KC)], R=[wb_tk, xT_tk[tt]], W=[p_tk])
                        epi(p, p_tk, tt)

                for u in range(NUQ):
                    kind = u // DNH
                    st, st_tk = stg.next()
                    prev = [None]

                    def epi(p, p_tk, tt, u=u, kind=kind, st=st, st_tk=st_tk, prev=prev):
                        rw, rw_tk = raw.next()
                        if tt == 0:
                            kb.op(DVE, lambda: V.memset(rw[:, 0:3], 0.0), W=[rw_tk])
                        else:
                            pr, pr_tk = prev[0]
                            kb.op(DVE, lambda: V.tensor_copy(out=rw[:, 0:3], in_=pr[:, 512:515]), R=[pr_tk],
                                  W=[rw_tk])
                        kb.op(ACT, lambda: A.copy(out=rw[:, 3:515], in_=p[:, :]), R=[p_tk], W=[rw_tk])
                        prev[0] = (rw, rw_tk)
                        ac, ac_tk = acc.next()
                        kb.op(DVE, lambda: V.tensor_scalar(out=ac[:], in0=rw[:, 0:512], scalar1=cw[:, u, 0:1],
                                                           scalar2=None, op0=ALU.mult), R=[rw_tk, cw_tk], W=[ac_tk])
                        for j in (1, 2, 3):
                            kb.op(DVE, lambda j=j: V.scalar_tensor_tensor(out=ac[:], in0=rw[:, j:j + 512],
                                                                          scalar=cw[:, u, j:j + 1], in1=ac[:],
                                                                          op0=ALU.mult, op1=ALU.add),
                                  R=[rw_tk, cw_tk, ac_tk], W=[ac_tk])
                        if kind == 2:
                            kb.op(ACT, lambda: A.activation(out=st[:, tt * 512:(tt + 1) * 512], in_=ac[:],
                                                            func=AF.Silu), R=[ac_tk], W=[st_tk])
                            return
                        s1, s1_tk = t1p.next()
                        kb.op(ACT, lambda: A.activation(out=s1[:], in_=ac[:], func=AF.Silu), R=[ac_tk], W=[s1_tk])
                        s2, s2_tk = t2p.next()
                        kb.op(DVE, lambda: V.tensor_tensor(out=s2[:], in0=s1[:], in1=s1[:], op=ALU.mult),
                              R=[s1_tk], W=[s2_tk])
                        q_, q_tk = pq.next()
                        kb.mm(q_[:, :], [(ones_f, s2[:])], R=[s2_tk, cst_tk], W=[q_tk])
                        kb.op(ACT, lambda: A.activation(out=s2[:], in_=q_[:, :], func=AF.Ln, bias=1e-6, scale=1.0),
                              R=[q_tk], W=[s2_tk])
                        kb.op(ACT, lambda: A.activation(out=s2[:], in_=s2[:], func=AF.Exp, scale=-0.5),
                              R=[s2_tk], W=[s2_tk])
                        sc = (128.0 ** -0.5) if kind == 0 else 1.0
                        kb.op(DVE, lambda: V.scalar_tensor_tensor(out=st[:, tt * 512:(tt + 1) * 512], in0=s1[:],
                                                                  scalar=sc, in1=s2[:], op0=ALU.mult, op1=ALU.mult),
                              R=[s1_tk, s2_tk], W=[st_tk])

                    fm_unit(c.oQKVA + u * 128, 128, epi)
                    kb.dma(ACT, qkvA[u], st[:], R=[st_tk], W=[tk["qkvA"]])

                for (col0, nun, dst, dkey, fn) in ((c.oZ, DNH, zT, "zT", AF.Silu), (c.oGA, KC, gAd, "gA", AF.Sigmoid),
                                                   (c.oGB, KC, gBd, "gB", AF.Sigmoid)):
                    for u in range(nun):
                        st, st_tk = stg.next()

                        def epi(p, p_tk, tt, st=st, st_tk=st_tk, fn=fn):
                            kb.op(ACT, lambda: A.activation(out=st[:, tt * 512:(tt + 1) * 512], in_=p[:, :], func=fn),
                                  R=[p_tk], W=[st_tk])

                        fm_unit(col0 + u * 128, 128, epi)
                        kb.dma(ACT, dst[u], st[:], R=[st_tk], W=[tk[dkey]])

                for i, col0 in enumerate((c.oA, c.oB)):
                    def epi(p, p_tk, tt, i=i):
                        sf, sf_tk = stfp.next()
                        kb.op(ACT, lambda: A.copy(out=sf[:], in_=p[0:DNH, :]), R=[p_tk], W=[sf_tk])
                        kb.dma(ACT, ab_d[i][:, tt * 512:(tt + 1) * 512], sf[:], R=[sf_tk], W=[tk["ab"]])

                    fm_unit(col0, DNH, epi)

                for qk in range(2):
                    for hd in range(NHB):
                        g = hd // DAH
                        dil = c.pats[g][1]
                        st, st_tk = stg.next()

                        def epi(p, p_tk, tt, st=st, st_tk=st_tk, dil=dil):
                            cs, cs_tk = csp.next()
                            kb.dma(SP, cs[:, 0, :], cos_in[:, tt * 512:(tt + 1) * 512], W=[cs_tk])
                            kb.dma(SP, cs[:, 1, :], sin_in[:, tt * 512:(tt + 1) * 512], W=[cs_tk])
                            t1, t1_tk = t1p.next()
                            kb.op(ACT, lambda: A.copy(out=t1[:], in_=p[:, :]), R=[p_tk], W=[t1_tk])
                            q_, q_tk = pq.next()
                            kb.mm(q_[:, :], [(rmat, t1[:])], R=[t1_tk, cst_tk], W=[q_tk])
                            t2, t2_tk = t2p.next()
                            kb.op(DVE, lambda: V.tensor_tensor(out=t2[:], in0=q_[:, :], in1=cs[:, 1, :], op=ALU.mult),
                                  R=[q_tk, cs_tk], W=[t2_tk])
                            kb.op(POOL, lambda: G.tensor_tensor(out=t1[:], in0=t1[:], in1=cs[:, 0, :], op=ALU.mult),
                                  R=[t1_tk, cs_tk], W=[t1_tk])
                            npr = 512 // dil
                            o = st[:, :].rearrange("p (r n) -> p r n", r=dil)[:, :, tt * npr:(tt + 1) * npr]
                            i0 = t1[:, :].rearrange("p (i r) -> p r i", r=dil)
                            i1 = t2[:, :].rearrange("p (i r) -> p r i", r=dil)
                            kb.op(DVE, lambda: V.tensor_tensor(out=o, in0=i0, in1=i1, op=ALU.add),
                                  R=[t1_tk, t2_tk], W=[st_tk])

                        fm_unit(c.oQKVB + (qk * NHB + hd) * 128, 128, epi)
                        kb.dma(ACT, (qB if qk == 0 else kBd)[hd], st[:], R=[st_tk], W=[tk["qB" if qk == 0 else "kB"]])

                for hd in range(NHB):
                    g = hd // DAH
                    dil = c.pats[g][1]
                    nb = S // (128 * dil)
                    wb, wb_tk = load_w(c.oQKVB + (2 * NHB + hd) * 128, 128)
                    st, st_tk = stg.next()
                    for b4 in range(NCH // 4):
                        p, p_tk = pp.next()
                        for j in range(4):
                            blk = b4 * 4 + j
                            r, n = divmod(blk, nb)
                            t0 = n * 128 * dil + r
                            kb.mm(p[:, j * 128:(j + 1) * 128],
                                  [(xT[:, k, t0:t0 + 127 * dil + 1:dil], wb[:, k, :]) for k in range(KC)],
                                  R=[wb_tk] + xT_tk, W=[p_tk])
                        kb.op(ACT, lambda: A.copy(out=st[:, b4 * 512:(b4 + 1) * 512], in_=p[:, :]), R=[p_tk],
                              W=[st_tk])
                    kb.dma(ACT, vB[hd], st[:], R=[st_tk], W=[tk["vB"]])
                kb.barrier()

        def phase_gdn(l):
            with contextlib.ExitStack() as es:
                HG = DBG.get("hg", 1)
                NQ = NCH * DNH
                colG = sb(es, "colG", [128, NQ])
                colB = sb(es, "colB", [128, NQ])
                colNB = sb(es, "colNB", [128, NQ])
                colEG = sb(es, "colEG", [128, NQ])
                colBEG = sb(es, "colBEG", [128, NQ])
                colEGL = sb(es, "colEGL", [128, NQ])
                colEL = sb(es, "colEL", [128, NQ])
                col_tk = TK()
                nwb = sb(es, "nwb", [128, 128])
                nwb_tk = TK()
                kb.dma(SP, nwb[:], dn_norm_w[l].partition_broadcast(128), W=[nwb_tk])
                with contextlib.ExitStack() as es2:
                    ga = sb(es2, "ga", [DNH, S])
                    gb = sb(es2, "gb", [DNH, S])
                    gc = sb(es2, "gc", [DNH, S])
                    sm = sb(es2, "sm", [DNH, S])
                    al = sb(es2, "al", [DNH, 2])
                    g_tk = TK()
                    kb.dma(SP, ga[:], ab_d[0], R=[tk["ab"]], W=[g_tk])
                    kb.dma(SP, gb[:], ab_d[1], R=[tk["ab"]], W=[g_tk])
                    kb.dma(SP, sm[:], scanmask_in[:, :], W=[g_tk])
                    kb.dma(SP, al[:, 0:1], a_log[l], W=[g_tk])
                    kb.dma(SP, al[:, 1:2], dt_bias[l], W=[g_tk])
                    kb.op(ACT, lambda: A.activation(out=ga[:], in_=ga[:], func=AF.Exp, bias=al[:, 1:2], scale=1.0),
                          R=[g_tk], W=[g_tk])
                    kb.op(ACT, lambda: A.activation(out=ga[:], in_=ga[:], func=AF.Ln, bias=1.0, scale=1.0),
                          R=[g_tk], W=[g_tk])
                    kb.op(ACT, lambda: A.activation(out=al[:, 0:1], in_=al[:, 0:1], func=AF.Exp), R=[g_tk], W=[g_tk])
                    kb.op(DVE, lambda: V.tensor_scalar(out=ga[:], in0=ga[:], scalar1=al[:, 0:1], scalar2=-1.0,
                                                       op0=ALU.mult, op1=ALU.mult), R=[g_tk], W=[g_tk])
                    kb.op(DVE, lambda: V.tensor_tensor_scan(out=gc[:], data0=sm[:], data1=ga[:], initial=0.0,
                                                            op0=ALU.mult, op1=ALU.add), R=[g_tk], W=[g_tk])
                    kb.op(ACT, lambda: A.activation(out=gb[:], in_=gb[:], func=AF.Sigmoid), R=[g_tk], W=[g_tk])
                    pg = ps(es2, "pg", [128, 512])[:, 0:NQ]
                    pb = ps(es2, "pb", [128, 512])[:, 0:NQ]
                    pl = ps(es2, "pl", [128, 512])[:, 0:NQ]
                    pg_tk = TK()
                    for n in range(NCH):
                        kb.op(PE, lambda n=n: T.transpose(pg[:, n * DNH:(n + 1) * DNH], gc[:, n * 128:(n + 1) * 128],
                                                          ident_f[0:DNH, 0:DNH]), R=[g_tk, cst_tk], W=[pg_tk])
                        kb.op(PE, lambda n=n: T.transpose(pb[:, n * DNH:(n + 1) * DNH], gb[:, n * 128:(n + 1) * 128],
                                                          ident_f[0:DNH, 0:DNH]), R=[g_tk, cst_tk], W=[pg_tk])
                    kb.op(ACT, lambda: A.copy(out=colG[:], in_=pg), R=[pg_tk], W=[col_tk])
                    kb.op(ACT, lambda: A.copy(out=colB[:], in_=pb), R=[pg_tk], W=[col_tk])
                    kb.mm(pl, [(sel127, colG[:])], R=[col_tk, cst_tk], W=[pg_tk])
                    kb.op(ACT, lambda: A.activation(out=colEL[:], in_=pl, func=AF.Exp), R=[pg_tk], W=[col_tk])
                    kb.op(DVE, lambda: V.tensor_tensor(out=colEGL[:], in0=pl, in1=colG[:], op=ALU.subtract),
                          R=[pg_tk, col_tk], W=[col_tk])
                    kb.op(ACT, lambda: A.activation(out=colEGL[:], in_=colEGL[:], func=AF.Exp), R=[col_tk], W=[col_tk])
                    kb.op(ACT, lambda: A.activation(out=colEG[:], in_=colG[:], func=AF.Exp), R=[col_tk], W=[col_tk])
                    kb.op(DVE, lambda: V.tensor_tensor(out=colBEG[:], in0=colEG[:], in1=colB[:], op=ALU.mult),
                          R=[col_tk], W=[col_tk])
                    kb.op(DVE, lambda: V.tensor_scalar(out=colNB[:], in0=colB[:], scalar1=-1.0, scalar2=None,
                                                       op0=ALU.mult), R=[col_tk], W=[col_tk])
                    kb.barrier()

                if DBG.get("gdn_stage", 99) == 0:
                    return
                hb = []
                for i in range(HG):
                    d = {}
                    for nm in ("q", "k", "v", "z", "y"):
                        d[nm] = sb(es, "h%s%d" % (nm, i), [128, S], BF16)
                        d[nm + "_tk"] = TK()
                    d["S"] = sb(es, "hS%d" % i, [128, 128])
                    d["Sb"] = sb(es, "hSb%d" % i, [128, 128], BF16)
                    d["S_tk"] = TK()
                    d["tmp"] = []
                    for par in range(DBG.get("npar", 2)):
                        t = {}
                        for nm in ("KBG", "K2", "BV", "ATd", "WT", "TT", "vn", "on", "Ya", "YaT", "Yb", "YbT", "P"):
                            t[nm] = sb(es, "t%s%d_%d" % (nm, i, par), [128, 128], BF16)
                            t[nm + "_tk"] = TK()
                        for nm in ("dg", "t1", "Ds", "t2", "DT", "U", "av", "o", "sq"):
                            t[nm] = sb(es, "t%s%d_%d" % (nm, i, par), [128, 128])
                            t[nm + "_tk"] = TK()
                        t["ss"] = sb(es, "tss%d_%d" % (i, par), [128, 2])
                        t["ss_tk"] = TK()
                        d["tmp"].append(t)
                    hb.append(d)
                for i in range(HG):
                    hb[i]["B"] = (ps(es, "gB%d" % i, [128, 1024], BF16), TK())
                    for j in range(3):
                        hb[i]["F%d" % j] = (ps(es, "gF%d_%d" % (j, i), [128, 512]), TK())

                def chunk_gen(hd, d, n):
                    t = d["tmp"][n % DBG.get("npar", 2)]
                    cs = slice(n * 128, (n + 1) * 128)
                    ci = n * DNH + hd
                    cG, cB, cNB = colG[:, ci:ci + 1], colB[:, ci:ci + 1], colNB[:, ci:ci + 1]
                    cEG, cBEG, cEGL, cEL = colEG[:, ci:ci + 1], colBEG[:, ci:ci + 1], colEGL[:, ci:ci + 1], \
                        colEL[:, ci:ci + 1]
                    kT, qT, vT = d["k"][:, cs], d["q"][:, cs], d["v"][:, cs]
                    p1, p1_tk = d["B"][0][:, 0:128], d["B"][1]
                    kb.op(PE, lambda: T.transpose(p1, kT, ident_b), R=[d["k_tk"], cstb_tk], W=[p1_tk])
                    p2, p2_tk = d["B"][0][:, 128:256], d["B"][1]
                    kb.op(PE, lambda: T.transpose(p2, vT, ident_b), R=[d["v_tk"], cstb_tk], W=[p2_tk])
                    p3, p3_tk = d["F0"][0][:, 0:128], d["F0"][1]
                    kb.mm(p3, [(kT, kT)], R=[d["k_tk"]], W=[p3_tk])
                    kb.op(DVE, lambda: V.tensor_scalar(out=t["dg"][:], in0=ident_f, scalar1=cG, scalar2=None,
                                                       op0=ALU.mult), R=[cst_tk, col_tk], W=[t["dg_tk"]])
                    yield
                    kb.op(ACT, lambda: A.activation(out=t["KBG"][:], in_=p1, func=AF.Copy, scale=cBEG),
                          R=[p1_tk, col_tk], W=[t["KBG_tk"]])
                    kb.op(DVE, lambda: V.tensor_scalar(out=t["K2"][:], in0=p1, scalar1=cEGL, scalar2=None,
                                                       op0=ALU.mult), R=[p1_tk, col_tk], W=[t["K2_tk"]])
                    kb.op(ACT, lambda: A.activation(out=t["BV"][:], in_=p2, func=AF.Copy, scale=cB),
                          R=[p2_tk, col_tk], W=[t["BV_tk"]])
                    p4, p4_tk = d["F1"][0][:, 0:128], d["F1"][1]
                    kb.mm(p4, [(ones_f, t["dg"][:])], R=[t["dg_tk"], cst_tk], W=[p4_tk])
                    yield
                    kb.op(DVE, lambda: V.scalar_tensor_tensor(out=t["t1"][:], in0=p4, scalar=-1.0, in1=maskS,
                                                              op0=ALU.mult, op1=ALU.add), R=[p4_tk, cst_tk],
                          W=[t["t1_tk"]])
                    kb.op(DVE, lambda: V.scalar_tensor_tensor(out=t["t2"][:], in0=p4, scalar=cG, in1=maskT,
                                                              op0=ALU.subtract, op1=ALU.add),
                          R=[p4_tk, cst_tk, col_tk], W=[t["t2_tk"]])
                    yield
                    kb.op(ACT, lambda: A.activation(out=t["Ds"][:], in_=t["t1"][:], func=AF.Exp, bias=cG, scale=1.0),
                          R=[t["t1_tk"], col_tk], W=[t["Ds_tk"]])
                    kb.op(ACT, lambda: A.activation(out=t["DT"][:], in_=t["t2"][:], func=AF.Exp), R=[t["t2_tk"]],
                          W=[t["DT_tk"]])
                    yield
                    kb.op(DVE, lambda: V.scalar_tensor_tensor(out=t["YaT"][:], in0=p3, scalar=cNB, in1=t["Ds"][:],
                                                              op0=ALU.mult, op1=ALU.mult),
                          R=[p3_tk, col_tk, t["Ds_tk"]], W=[t["YaT_tk"]])
                    p5, p5_tk = d["B"][0][:, 256:384], d["B"][1]
                    kb.op(PE, lambda: T.transpose(p5, t["YaT"][:], ident_b), R=[t["YaT_tk"], cstb_tk], W=[p5_tk])
                    p6, p6_tk = d["F2"][0][:, 0:128], d["F2"][1]
                    kb.mm(p6, [(kT, qT)], R=[d["k_tk"], d["q_tk"]], W=[p6_tk])
                    yield
                    sk = DBG.get("skip", [])
                    if 0 not in sk:
                        kb.op(ACT, lambda: A.copy(out=t["Ya"][:], in_=p5), R=[p5_tk], W=[t["Ya_tk"]])
                    if 1 not in sk:
                        kb.op(POOL, lambda: G.tensor_tensor(out=t["P"][:], in0=t["Ya"][:], in1=ident_f, op=ALU.add),
                              R=[t["Ya_tk"], cst_tk], W=[t["P_tk"]])
                    if 2 not in sk:
                        kb.op(DVE, lambda: V.tensor_tensor(out=t["ATd"][:], in0=p6, in1=t["DT"][:], op=ALU.mult),
                              R=[p6_tk, t["DT_tk"]], W=[t["ATd_tk"]])
                    if hd == 0 and n in DBG.get("dump6", []):
                        for nm in DBG.get("dump6_names", []):
                            dump("%s_%d" % (nm, n), t[nm][:], t[nm + "_tk"])
                        dump("colG", colG[:, 0:NQ if NQ < 128 else 128], col_tk, 128, min(NQ, 128))
                        dump("colB", colB[:, 0:NQ if NQ < 128 else 128], col_tk, 128, min(NQ, 128))
                        dump("colEGL", colEGL[:, 0:NQ if NQ < 128 else 128], col_tk, 128, min(NQ, 128))
                        dump("colEL", colEL[:, 0:NQ if NQ < 128 else 128], col_tk, 128, min(NQ, 128))
                    yield
                    cur = ("Ya", "YaT")
                    nxt = ("Yb", "YbT")
                    for it in range(6):
                        Y, YT = t[cur[0]], t[cur[1]]
                        Y_tk, YT_tk = t[cur[0] + "_tk"], t[cur[1] + "_tk"]
                        N_, NT = t[nxt[0]], t[nxt[1]]
                        N_tk, NT_tk = t[nxt[0] + "_tk"], t[nxt[1] + "_tk"]
                        last = it == 5
                        pa = None
                        if not last:
                            pa, pa_tk = d["F0"][0][:, 0:128], d["F0"][1]
                            kb.mm(pa, [(YT[:], Y[:])], R=[Y_tk, YT_tk], W=[pa_tk])
                        pbt, pbt_tk = d["F1"][0][:, 0:128], d["F1"][1]
                        kb.mm(pbt, [(Y[:], YT[:])], R=[Y_tk, YT_tk], W=[pbt_tk])
                        yield
                        if not last:
                            kb.op(ACT, lambda: A.copy(out=N_[:], in_=pa), R=[pa_tk], W=[N_tk])
                        kb.op(DVE, lambda: V.tensor_copy(out=NT[:], in_=pbt), R=[pbt_tk], W=[NT_tk])
                        pc, pc_tk = d["F2"][0][:, 0:128], d["F2"][1]
                        kb.mm(pc, [(NT[:], t["P"][:])], R=[NT_tk, t["P_tk"]], W=[pc_tk])
                        yield
                        if last:
                            kb.op(DVE, lambda: V.tensor_tensor(out=t["TT"][:], in0=pc, in1=t["P"][:], op=ALU.add),
                                  R=[pc_tk, t["P_tk"]], W=[t["TT_tk"]])
                        else:
                            kb.op(DVE, lambda: V.tensor_tensor(out=t["P"][:], in0=pc, in1=t["P"][:], op=ALU.add),
                                  R=[pc_tk, t["P_tk"]], W=[t["P_tk"]])
                        cur, nxt = nxt, cur
                        yield
                    p7, p7_tk = d["F0"][0][:, 0:128], d["F0"][1]
                    kb.mm(p7, [(t["KBG"][:], t["TT"][:])], R=[t["KBG_tk"], t["TT_tk"]], W=[p7_tk])
                    p8, p8_tk = d["F1"][0][:, 0:128], d["F1"][1]
                    kb.mm(p8, [(t["TT"][:], t["BV"][:])], R=[t["TT_tk"], t["BV_tk"]], W=[p8_tk])
                    yield
                    kb.op(ACT, lambda: A.copy(out=t["WT"][:], in_=p7), R=[p7_tk], W=[t["WT_tk"]])
                    kb.op(ACT, lambda: A.copy(out=t["U"][:], in_=p8), R=[p8_tk], W=[t["U_tk"]])
                    yield
                    p9, p9_tk = d["F2"][0][:, 0:128], d["F2"][1]
                    kb.mm(p9, [(t["WT"][:], d["Sb"][:])], R=[t["WT_tk"], d["S_tk"]], W=[p9_tk])
                    p10, p10_tk = d["F0"][0][:, 0:128], d["F0"][1]
                    kb.mm(p10, [(qT, d["Sb"][:])], R=[d["q_tk"], d["S_tk"]], W=[p10_tk])
                    yield
                    kb.op(DVE, lambda: V.tensor_tensor(out=t["vn"][:], in0=t["U"][:], in1=p9, op=ALU.subtract),
                          R=[t["U_tk"], p9_tk], W=[t["vn_tk"]])
                    yield
                    p11, p11_tk = d["F1"][0][:, 0:128], d["F1"][1]
                    kb.mm(p11, [(t["ATd"][:], t["vn"][:])], R=[t["ATd_tk"], t["vn_tk"]], W=[p11_tk])
                    p12, p12_tk = d["F2"][0][:, 0:128], d["F2"][1]
                    kb.mm(p12, [(t["K2"][:], t["vn"][:])], R=[t["K2_tk"], t["vn_tk"]], W=[p12_tk])
                    yield
                    kb.op(ACT, lambda: A.copy(out=t["av"][:], in_=p11), R=[p11_tk], W=[t["av_tk"]])
                    kb.op(DVE, lambda: V.scalar_tensor_tensor(out=d["S"][:], in0=d["S"][:], scalar=cEL, in1=p12,
                                                              op0=ALU.mult, op1=ALU.add),
                          R=[d["S_tk"], col_tk, p12_tk], W=[d["S_tk"]])
                    kb.op(ACT, lambda: A.copy(out=d["Sb"][:], in_=d["S"][:]), R=[d["S_tk"]], W=[d["S_tk"]])
                    yield
                    kb.op(DVE, lambda: V.scalar_tensor_tensor(out=t["o"][:], in0=p10, scalar=cEG, in1=t["av"][:],
                                                              op0=ALU.mult, op1=ALU.add),
                          R=[p10_tk, col_tk, t["av_tk"]], W=[t["o_tk"]])
                    kb.op(ACT, lambda: A.activation(out=t["sq"][:], in_=t["o"][:], func=AF.Square,
                                                    accum_out=t["ss"][:, 0:1]), R=[t["o_tk"]],
                          W=[t["sq_tk"], t["ss_tk"]])
                    kb.op(ACT, lambda: A.activation(out=t["ss"][:, 1:2], in_=t["ss"][:, 0:1], func=AF.Ln, bias=1e-6,
                                                    scale=1.0 / 128.0), R=[t["ss_tk"]], W=[t["ss_tk"]])
                    kb.op(ACT, lambda: A.activation(out=t["ss"][:, 1:2], in_=t["ss"][:, 1:2], func=AF.Exp, scale=-0.5),
                          R=[t["ss_tk"]], W=[t["ss_tk"]])
                    yield
                    kb.op(DVE, lambda: V.scalar_tensor_tensor(out=t["on"][:], in0=t["o"][:], scalar=t["ss"][:, 1:2],
                                                              in1=nwb[:], op0=ALU.mult, op1=ALU.mult),
                          R=[t["o_tk"], t["ss_tk"], nwb_tk], W=[t["on_tk"]])
                    p13, p13_tk = d["B"][0][:, 0:128], d["B"][1]
                    kb.op(PE, lambda: T.transpose(p13, t["on"][:], ident_b), R=[t["on_tk"], cstb_tk], W=[p13_tk])
                    yield
                    kb.op(DVE, lambda: V.tensor_tensor(out=d["y"][:, cs], in0=p13, in1=d["z"][:, cs], op=ALU.mult),
                          R=[p13_tk, d["z_tk"]], W=[d["y_tk"]])
                    if hd == 0 and n in DBG.get("dump_chunks", []):
                        for nm in ("KBG", "K2", "BV", "Ds", "DT", "TT", "ATd", "WT", "U", "vn", "av", "o", "on"):
                            dump("%s_%d" % (nm, n), t[nm][:], t[nm + "_tk"])
                        dump("S_%d" % n, d["S"][:], d["S_tk"])
                        dump("y_%d" % n, d["y"][:, cs], d["y_tk"])
                        dump("q_%d" % n, qT, d["q_tk"])
                        dump("k_%d" % n, kT, d["k_tk"])
                    yield

                for h0 in range(0, DNH, HG):
                    for i in range(HG):
                        hd = h0 + i
                        d = hb[i]
                        kb.dma(SP, d["q"][:], qkvA[hd], R=[tk["qkvA"]], W=[d["q_tk"]])
                        kb.dma(SP, d["k"][:], qkvA[DNH + hd], R=[tk["qkvA"]], W=[d["k_tk"]])
                        kb.dma(SP, d["v"][:], qkvA[2 * DNH + hd], R=[tk["qkvA"]], W=[d["v_tk"]])
                        kb.dma(SP, d["z"][:], zT[hd], R=[tk["zT"]], W=[d["z_tk"]])
                        kb.op(DVE, lambda d=d: V.memset(d["S"][:], 0.0), W=[d["S_tk"]])
                        kb.op(DVE, lambda d=d: V.memset(d["Sb"][:], 0.0), W=[d["S_tk"]])
                    for n in range(min(NCH, DBG.get("gdn_chunks", NCH))):
                        gens = [itertools.islice(chunk_gen(h0 + i, hb[i], n), DBG.get("gdn_stage", 99))
                                for i in range(HG)]
                        for _ in itertools.zip_longest(*gens):
                            pass
                    for i in range(HG):
                        kb.dma(ACT, yAd[h0 + i], hb[i]["y"][:], R=[hb[i]["y_tk"]], W=[tk["yA"]])
                kb.barrier()

        def phase_attn(l):
            with contextlib.ExitStack() as es:
                am_f = sb(es, "am_f", [128, 256])
                am = sb(es, "am", [128, 256], BF16)
                am_tk = TK()
                kb.dma(SP, am_f[:], amask_in[:, :], W=[am_tk])
                kb.op(DVE, lambda: V.tensor_copy(out=am[:], in_=am_f[:]), R=[am_tk], W=[am_tk])
                qp = Pool([sb(es, "aq%d" % i, [128, S], BF16) for i in range(2)])
                kp = Pool([sb(es, "ak%d" % i, [128, S], BF16) for i in range(2)])
                vp = Pool([sb(es, "av%d" % i, [128, S], BF16) for i in range(2)])
                accp = Pool([sb(es, "aacc%d" % i, [128, 2, S]) for i in range(2)])
                ptp = Pool([sb(es, "apt%d" % i, [128, 256], BF16) for i in range(3)])
                yst = Pool([sb(es, "ayst%d" % i, [128, S], BF16) for i in range(2)])
                pss = Pool([ps(es, "aps%d" % i, [128, 512])[:, 0:256] for i in range(4)])
                pso = Pool([ps(es, "apo%d" % i, [128, 512])[:, 0:256].rearrange("p (a b) -> p a b", a=2) for i in range(4)])
                esc = 128.0 ** -0.5
                for slot in range(DAH):
                    ac, ac_tk = accp.next()
                    for g in range(NG):
                        hd = g * DAH + slot
                        dil = c.pats[g][1]
                        nb = S // (128 * dil)
                        q_, q_tk = qp.next()
                        k_, k_tk = kp.next()
                        v_, v_tk = vp.next()
                        kb.dma(SP, q_[:], qB[hd], R=[tk["qB"]], W=[q_tk])
                        kb.dma(SP, k_[:], kBd[hd], R=[tk["kB"]], W=[k_tk])
                        kb.dma(SP, v_[:], vB[hd], R=[tk["vB"]], W=[v_tk])
                        for blk in range(NCH):
                            r, n = divmod(blk, nb)
                            cq = slice(blk * 128, (blk + 1) * 128)
                            cp = slice((blk - 1) * 128, blk * 128)
                            s_, s_tk = pss.next()
                            pt, pt_tk = ptp.next()
                            o_, o_tk = pso.next()
                            if n > 0:
                                kb.mm(s_[:, 0:128], [(k_[:, cp], q_[:, cq])], R=[k_tk, q_tk], W=[s_tk])
                                kb.mm(s_[:, 128:256], [(k_[:, cq], q_[:, cq])], R=[k_tk, q_tk], W=[s_tk])
                                lo = 0
                            else:
                                kb.mm(s_[:, 128:256], [(k_[:, cq], q_[:, cq])], R=[k_tk, q_tk], W=[s_tk])
                                lo = 128
                            kb.op(ACT, lambda: A.activation(out=pt[:, lo:256], in_=s_[:, lo:256], func=AF.Exp,
                                                            scale=esc), R=[s_tk], W=[pt_tk])
                            kb.op(POOL, lambda: G.tensor_tensor(out=pt[:, lo:256], in0=pt[:, lo:256],
                                                                in1=am[:, lo:256], op=ALU.mult), R=[pt_tk, am_tk],
                                  W=[pt_tk])
                            if n > 0:
                                kb.mm(o_[:, 0, :], [(v_[:, cp], pt[:, 0:128]), (v_[:, cq], pt[:, 128:256])],
                                      R=[v_tk, pt_tk], W=[o_tk])
                                kb.mm(o_[:, 1, :], [(ones_b, pt[:, 0:128]), (ones_b, pt[:, 128:256])],
                                      R=[pt_tk, cstb_tk], W=[o_tk])
                            else:
                                kb.mm(o_[:, 0, :], [(v_[:, cq], pt[:, 128:256])], R=[v_tk, pt_tk], W=[o_tk])
                                kb.mm(o_[:, 1, :], [(ones_b, pt[:, 128:256])], R=[pt_tk, cstb_tk], W=[o_tk])
                            t0 = n * 128 * dil + r
                            dst = ac[:, :, t0:t0 + 127 * dil + 1:dil]
                            if g == 0:
                                kb.op(DVE, lambda: V.tensor_copy(out=dst, in_=o_[:, :, :]), R=[o_tk], W=[ac_tk])
                            else:
                                kb.op(DVE, lambda: V.tensor_tensor(out=dst, in0=o_[:, :, :], in1=dst, op=ALU.add),
                                      R=[o_tk, ac_tk], W=[ac_tk])
                    ys, ys_tk = yst.next()
                    kb.op(DVE, lambda: V.reciprocal(out=ac[:, 1, :], in_=ac[:, 1, :]), R=[ac_tk], W=[ac_tk])
                    kb.op(POOL, lambda: G.tensor_tensor(out=ys[:], in0=ac[:, 0, :], in1=ac[:, 1, :], op=ALU.mult),
                          R=[ac_tk], W=[ys_tk])
                    kb.dma(ACT, yBd[slot], ys[:], R=[ys_tk], W=[tk["yB"]])
                kb.barrier()

        def load_ln(es, g_ap, b_ap, nm):
            g = sb(es, nm + "g", [128, D])
            b = sb(es, nm + "b", [128, D])
            t_ = TK()
            kb.dma(SP, g[:], g_ap.partition_broadcast(128), W=[t_])
            kb.dma(SP, b[:], b_ap.partition_broadcast(128), W=[t_])
            return g, b, t_

        def layer_norm(es_tmp, h, h_tk, g, b, gb_tk, stats, mv, st_tk, out, out_tk):
            nchunk = D // 512 if D >= 512 else 1
            w = D // nchunk
            for i in range(nchunk):
                kb.op(DVE, lambda i=i: V.bn_stats(out=stats[:, i * 6:(i + 1) * 6], in_=h[:, i * w:(i + 1) * w]),
                      R=[h_tk], W=[st_tk])
            kb.op(DVE, lambda: V.bn_aggr(out=mv[:, 0:2], in_=stats[:, 0:nchunk * 6]), R=[st_tk], W=[st_tk])
            kb.op(ACT, lambda: A.activation(out=mv[:, 2:3], in_=mv[:, 1:2], func=AF.Ln, bias=1e-5, scale=1.0),
                  R=[st_tk], W=[st_tk])
            kb.op(ACT, lambda: A.activation(out=mv[:, 2:3], in_=mv[:, 2:3], func=AF.Exp, scale=-0.5), R=[st_tk],
                  W=[st_tk])
            kb.op(DVE, lambda: V.tensor_scalar(out=h[:], in0=h[:], scalar1=mv[:, 0:1], scalar2=mv[:, 2:3],
                                               op0=ALU.subtract, op1=ALU.mult), R=[h_tk, st_tk], W=[h_tk])
            kb.op(POOL, lambda: G.tensor_tensor(out=h[:], in0=h[:], in1=g[:], op=ALU.mult), R=[h_tk, gb_tk], W=[h_tk])
            kb.op(POOL, lambda: G.tensor_tensor(out=out[:], in0=h[:], in1=b[:], op=ALU.add), R=[h_tk, gb_tk],
                  W=[out_tk] if out_tk is not h_tk else [h_tk])

        def load_weight_resident(w, es_tmp, name, src, nk, ncols):
            w_tk = TK()
            stp = Pool([sb(es_tmp, name + "_st%d" % i, [128, ncols]) for i in range(2)])
            for k in range(nk):
                s_, s_tk = stp.next()
                kb.dma(SP, s_[:], src[k * 128:(k + 1) * 128, :], W=[s_tk])
                kb.op(POOL, lambda k=k, s_=s_: G.tensor_copy(out=w[:, k, :], in_=s_[:]), R=[s_tk], W=[w_tk])
            return w, w_tk

        def phase_merge(l, src, src_tk, dst, dst_tk):
            KA, KBn = c.DNW // 128, c.DAW // 128
            with contextlib.ExitStack() as es:
                wa = sb(es, "wa", [128, KA, D], BF16)
                wbb = sb(es, "wbb", [128, KBn, D], BF16)
                with contextlib.ExitStack() as es2:
                    wa, wa_tk = load_weight_resident(wa, es2, "wa", w_ba[l], KA, D)
                    wbb, wbb_tk = load_weight_resident(wbb, es2, "wbb", w_bb[l], KBn, D)
                    kb.barrier()
                yap = Pool([sb(es, "m_ya%d" % i, [128, KA, 512], BF16) for i in range(1)])
                ybp = Pool([sb(es, "m_yb%d" % i, [128, KBn, 512], BF16) for i in range(1)])
                gap = Pool([sb(es, "m_ga%d" % i, [128, KC, 512], BF16) for i in range(1)])
                gbp = Pool([sb(es, "m_gb%d" % i, [128, KC, 512], BF16) for i in range(1)])
                mop = Pool([sb(es, "m_mo%d" % i, [128, KC, 512], BF16) for i in range(2)])
                tmp = Pool([sb(es, "m_t%d" % i, [128, 512]) for i in range(2)])
                ppa = Pool([ps(es, "m_pa%d" % i, [128, 512]) for i in range(3)])
                ppb = Pool([ps(es, "m_pb%d" % i, [128, 512]) for i in range(3)])
                for tt in range(NT5):
                    ts_ = slice(tt * 512, (tt + 1) * 512)
                    ya, ya_tk = yap.next()
                    yb, yb_tk = ybp.next()
                    ga, ga_tk = gap.next()
                    gb_, gb_tk = gbp.next()
                    mo, mo_tk = mop.next()
                    kb.dma(SP, ya[:], yAd[:, :, ts_].rearrange("k p t -> p k t"), R=[tk["yA"]], W=[ya_tk])
                    kb.dma(SP, yb[:], yBd[:, :, ts_].rearrange("k p t -> p k t"), R=[tk["yB"]], W=[yb_tk])
                    kb.dma(SP, ga[:], gAd[:, :, ts_].rearrange("k p t -> p k t"), R=[tk["gA"]], W=[ga_tk])
                    kb.dma(SP, gb_[:], gBd[:, :, ts_].rearrange("k p t -> p k t"), R=[tk["gB"]], W=[gb_tk])
                    for u in range(KC):
                        us = slice(u * 128, (u + 1) * 128)
                        pa, pa_tk = ppa.next()
                        kb.mm(pa[:, :], [(wa[:, k, us], ya[:, k, :]) for k in range(KA)], R=[wa_tk, ya_tk], W=[pa_tk])
                        pb_, pb_tk = ppb.next()
                        kb.mm(pb_[:, :], [(wbb[:, k, us], yb[:, k, :]) for k in range(KBn)], R=[wbb_tk, yb_tk],
                              W=[pb_tk])
                        t_, t_tk = tmp.next()
                        kb.op(DVE, lambda: V.tensor_tensor(out=t_[:], in0=pa[:, :], in1=ga[:, u, :], op=ALU.mult),
                              R=[pa_tk, ga_tk], W=[t_tk])
                        kb.op(DVE, lambda: V.tensor_tensor(out=mo[:, u, :], in0=pb_[:, :], in1=gb_[:, u, :],
                                                           op=ALU.mult), R=[pb_tk, gb_tk], W=[mo_tk])
                        kb.op(POOL, lambda: G.tensor_tensor(out=mo[:, u, :], in0=mo[:, u, :], in1=t_[:], op=ALU.add),
                              R=[t_tk, mo_tk], W=[mo_tk])
                    kb.dma(ACT, mrg[:, :, ts_].rearrange("k p t -> p k t"), mo[:], R=[mo_tk], W=[tk["mrg"]])
                kb.barrier()
            with contextlib.ExitStack() as es:
                wo = sb(es, "wo", [128, KC, D], BF16)
                with contextlib.ExitStack() as es2:
                    wo, wo_tk = load_weight_resident(wo, es2, "wo", w_out[l], KC, D)
                    kb.barrier()
                g, b, gb_tk = load_ln(es, ln1_g[l], ln1_b[l], "ln1")
                mip = Pool([sb(es, "o_mi%d" % i, [128, KC, 128], BF16) for i in range(2)])
                xp = Pool([sb(es, "o_x%d" % i, [128, D]) for i in range(2)])
                hp = Pool([sb(es, "o_h%d" % i, [128, D]) for i in range(2)])
                stats = sb(es, "o_stats", [128, 6 * 8])
                mv = sb(es, "o_mv", [128, 4])
                st_tk = TK()
                NB = (D + 511) // 512
                pm = Pool([ps(es, "o_pm%d" % i, [128, 512]) for i in range(min(8, 2 * NB))])
                for tt in range(S // 128):
                    ts_ = slice(tt * 128, (tt + 1) * 128)
                    mi, mi_tk = mip.next()
                    x_, x_tk = xp.next()
                    h, h_tk = hp.next()
                    kb.dma(SP, mi[:], mrg[:, :, ts_].rearrange("k p t -> p k t"), R=[tk["mrg"]], W=[mi_tk])
                    kb.dma(SP, x_[:], src[ts_, :], R=[src_tk], W=[x_tk])
                    for cb in range(NB):
                        w = min(512, D - cb * 512)
                        cs = slice(cb * 512, cb * 512 + w)
                        p, p_tk = pm.next()
                        kb.mm(p[:, 0:w], [(mi[:, k, :], wo[:, k, cs]) for k in range(KC)], R=[mi_tk, wo_tk], W=[p_tk])
                        kb.op(DVE, lambda: V.scalar_tensor_tensor(out=h[:, cs], in0=x_[:, cs], scalar=float(c.alpha),
                                                                  in1=p[:, 0:w], op0=ALU.mult, op1=ALU.add),
                              R=[x_tk, p_tk], W=[h_tk])
                    layer_norm(es, h, h_tk, g, b, gb_tk, stats, mv, st_tk, h, h_tk)
                    kb.dma(ACT, dst[ts_, :], h[:], R=[h_tk], W=[dst_tk])
                kb.barrier()

        def phase_moe(l, src, src_tk, dst, dst_tk):
            with contextlib.ExitStack() as es:
                xT, xT_tk = phase_transpose(es, src, src_tk)
                combT = sb(es, "combT", [NE, S])
                combT_tk = TK()
                with contextlib.ExitStack() as es2:
                    wr_f = sb(es2, "wr_f", [128, KC, NE])
                    wr = sb(es2, "wr", [128, KC, NE], BF16)
                    rb = sb(es2, "rb", [128, NE])
                    wr_tk = TK()
                    kb.dma(SP, wr_f[:], router_w[l].rearrange("(k p) e -> p k e", p=128), W=[wr_tk])
                    kb.dma(SP, rb[:], router_b[l].partition_broadcast(128), W=[wr_tk])
                    kb.op(DVE, lambda: V.tensor_copy(out=wr[:], in_=wr_f[:]), R=[wr_tk], W=[wr_tk])
                    lg = Pool([sb(es2, "lg%d" % i, [128, NE]) for i in range(2)])
                    ex = Pool([sb(es2, "ex%d" % i, [128, NE]) for i in range(2)])
                    t8 = Pool([sb(es2, "t8%d" % i, [128, 16]) for i in range(2)])
                    prr = Pool([ps(es2, "prr%d" % i, [128, 512])[:, 0:NE] for i in range(2)])
                    prt = Pool([ps(es2, "prt%d" % i, [128, 512])[0:NE, 0:128] for i in range(2)])
                    for tt in range(S // 128):
                        ts_ = slice(tt * 128, (tt + 1) * 128)
                        p, p_tk = prr.next()
                        kb.mm(p[:, :], [(xT[:, k, ts_], wr[:, k, :]) for k in range(KC)], R=[wr_tk, xT_tk[tt // 4]],
                              W=[p_tk])
                        lgt, lg_tk = lg.next()
                        kb.op(DVE, lambda: V.tensor_tensor(out=lgt[:], in0=p[:, :], in1=rb[:], op=ALU.add),
                              R=[p_tk, wr_tk], W=[lg_tk])
                        t_, t_tk = t8.next()
                        kb.op(DVE, lambda: V.max(out=t_[:, 0:8], in_=lgt[:]), R=[lg_tk], W=[t_tk])
                        kb.op(DVE, lambda: V.tensor_scalar(out=t_[:, 8:9], in0=t_[:, 0:1], scalar1=-1.0, scalar2=None,
                                                           op0=ALU.mult), R=[t_tk], W=[t_tk])
                        e_, e_tk = ex.next()
                        kb.op(ACT, lambda: A.activation(out=e_[:], in_=lgt[:], func=AF.Exp, bias=t_[:, 8:9],
                                                        scale=1.0), R=[lg_tk, t_tk], W=[e_tk])
                        kb.op(DVE, lambda: V.tensor_scalar(out=lgt[:], in0=lgt[:], scalar1=t_[:, 3:4], scalar2=None,
                                                           op0=ALU.is_ge), R=[lg_tk, t_tk], W=[lg_tk])
                        kb.op(DVE, lambda: V.tensor_tensor(out=e_[:], in0=e_[:], in1=lgt[:], op=ALU.mult),
                              R=[lg_tk, e_tk], W=[e_tk])
                        kb.op(DVE, lambda: V.reduce_sum(out=t_[:, 9:10], in_=e_[:], axis=AX.X), R=[e_tk], W=[t_tk])
                        kb.op(DVE, lambda: V.reciprocal(out=t_[:, 10:11], in_=t_[:, 9:10]), R=[t_tk], W=[t_tk])
                        kb.op(DVE, lambda: V.tensor_scalar(out=e_[:], in0=e_[:], scalar1=t_[:, 10:11], scalar2=None,
                                                           op0=ALU.mult), R=[e_tk, t_tk], W=[e_tk])
                        pt, pt_tk = prt.next()
                        kb.op(PE, lambda: T.transpose(pt[:, :], e_[:], ident_f), R=[e_tk, cst_tk], W=[pt_tk])
                        kb.op(ACT, lambda: A.copy(out=combT[:, ts_], in_=pt[:, :]), R=[pt_tk], W=[combT_tk])
                    kb.dma(ACT, combT_d[:, :], combT[:], R=[combT_tk], W=[tk["combT"]])
                    kb.barrier()
                sele = sb(es, "sele", [NE, NE * 128])
                bgu = sb(es, "bgu", [128, NE * 2 * EC])
                se_tk = TK()
                kb.dma(SP, sele[:], sele_in[:, :], W=[se_tk])
                kb.dma(SP, bgu[:], b_gu[l], W=[se_tk])
                wst = Pool([sb(es, "e_wst%d" % i, [128, KC, 128]) for i in range(2)])
                wbf = Pool([sb(es, "e_wbf%d" % i, [128, KC, 128], BF16) for i in range(3)])
                gg = Pool([sb(es, "e_g%d" % i, [128, 512]) for i in range(2)])
                sg = Pool([sb(es, "e_s%d" % i, [128, 512]) for i in range(2)])
                uu = Pool([sb(es, "e_u%d" % i, [128, 512]) for i in range(2)])
                hst = Pool([sb(es, "e_h%d" % i, [128, 512], BF16) for i in range(2)])
                pg_ = Pool([ps(es, "e_pg%d" % i, [128, 512]) for i in range(3)])
                pu_ = Pool([ps(es, "e_pu%d" % i, [128, 512]) for i in range(3)])
                pc_ = Pool([ps(es, "e_pc%d" % i, [128, 512]) for i in range(2)])

                def load_w(e, col0):
                    ws, ws_tk = wst.next()
                    kb.dma(SP, ws[:], w_gu[l, e, :, col0:col0 + 128].rearrange("(k p) c -> p k c", p=128), W=[ws_tk])
                    wb, wb_tk = wbf.next()
                    kb.op(POOL, lambda: G.tensor_copy(out=wb[:], in_=ws[:]), R=[ws_tk], W=[wb_tk])
                    return wb, wb_tk

                for e in range(NE):
                    for j in range(EC):
                        wg, wg_tk = load_w(e, j * 128)
                        wu, wu_tk = load_w(e, DE + j * 128)
                        bg = bgu[:, e * 2 * EC + j:e * 2 * EC + j + 1]
                        bu = bgu[:, e * 2 * EC + EC + j:e * 2 * EC + EC + j + 1]
                        for tt in range(NT5):
                            ts_ = slice(tt * 512, (tt + 1) * 512)
                            hs, hs_tk = hst.next()
                            p1, p1_tk = pg_.next()
                            kb.mm(p1[:, :], [(wg[:, k, :], xT[:, k, ts_]) for k in range(KC)], R=[wg_tk, xT_tk[tt]],
                                  W=[p1_tk])
                            p2, p2_tk = pu_.next()
                            kb.mm(p2[:, :], [(wu[:, k, :], xT[:, k, ts_]) for k in range(KC)], R=[wu_tk, xT_tk[tt]],
                                  W=[p2_tk])
                            p3, p3_tk = pc_.next()
                            kb.mm(p3[:, :], [(sele[:, e * 128:(e + 1) * 128], combT[:, ts_])], R=[se_tk, combT_tk],
                                  W=[p3_tk])
                            g_, g_tk = gg.next()
                            s_, s_tk = sg.next()
                            u_, u_tk = uu.next()
                            kb.op(DVE, lambda: V.tensor_scalar(out=g_[:], in0=p1[:, :], scalar1=bg, scalar2=7.0,
                                                               op0=ALU.add, op1=ALU.min), R=[p1_tk, se_tk], W=[g_tk])
                            kb.op(ACT, lambda: A.activation(out=s_[:], in_=g_[:], func=AF.Sigmoid, scale=1.702),
                                  R=[g_tk], W=[s_tk])
                            kb.op(DVE, lambda: V.tensor_scalar(out=u_[:], in0=p2[:, :], scalar1=bu, scalar2=7.0,
                                                               op0=ALU.add, op1=ALU.min), R=[p2_tk, se_tk], W=[u_tk])
                            kb.op(POOL, lambda: G.tensor_scalar(out=u_[:], in0=u_[:], scalar1=-7.0, scalar2=1.0,
                                                                op0=ALU.max, op1=ALU.add), R=[u_tk], W=[u_tk])
                            kb.op(DVE, lambda: V.tensor_tensor(out=g_[:], in0=g_[:], in1=s_[:], op=ALU.mult),
                                  R=[g_tk, s_tk], W=[g_tk])
                            kb.op(DVE, lambda: V.tensor_tensor(out=u_[:], in0=p3[:, :], in1=u_[:], op=ALU.mult),
                                  R=[p3_tk, u_tk], W=[u_tk])
                            kb.op(DVE, lambda: V.tensor_tensor(out=hs[:], in0=g_[:], in1=u_[:], op=ALU.mult),
                                  R=[g_tk, u_tk], W=[hs_tk])
                            kb.dma(ACT, hT[e * EC + j][:, ts_], hs[:], R=[hs_tk], W=[tk["hT"]])
                kb.barrier()
            with contextlib.ExitStack() as es:
                TB = min(1024, S)
                NSB = TB // 128
                NB = (D + 511) // 512
                yacc = sb(es, "yacc", [128, NSB, D])
                yacc_tk = [TK() for _ in range(NSB)]
                g, b, gb_tk = load_ln(es, ln2_g[l], ln2_b[l], "ln2")
                bd = sb(es, "bd", [NE, D])
                bd_tk = TK()
                kb.dma(SP, bd[:], b_dn[l], W=[bd_tk])
                cT = sb(es, "cT", [NE, TB])
                cT_tk = TK()
                xp = Pool([sb(es, "d_x%d" % i, [128, D]) for i in range(2)])
                hp = Pool([sb(es, "d_h%d" % i, [128, EC, TB], BF16) for i in range(2)])
                wsp = Pool([sb(es, "d_ws%d" % i, [128, D]) for i in range(2)])
                wdp = Pool([sb(es, "d_wd%d" % i, [128, EC, D], BF16) for i in range(2)])
                stats = sb(es, "d_stats", [128, 6 * 8])
                mv = sb(es, "d_mv", [128, 4])
                st_tk = TK()
                pm = Pool([ps(es, "d_pm%d" % i, [128, 512]) for i in range(6)])
                for tb in range(S // TB):
                    t0 = tb * TB
                    kb.dma(SP, cT[:], combT_d[:, t0:t0 + TB], R=[tk["combT"]], W=[cT_tk])
                    for sbt in range(NSB):
                        x_, x_tk = xp.next()
                        kb.dma(SP, x_[:], src[t0 + sbt * 128:t0 + (sbt + 1) * 128, :], R=[src_tk], W=[x_tk])
                        for cb in range(NB):
                            w = min(512, D - cb * 512)
                            cs = slice(cb * 512, cb * 512 + w)
                            p, p_tk = pm.next()
                            kb.mm(p[:, 0:w], [(cT[:, sbt * 128:(sbt + 1) * 128], bd[:, cs])], R=[cT_tk, bd_tk],
                                  W=[p_tk])
                            kb.op(DVE, lambda: V.scalar_tensor_tensor(out=yacc[:, sbt, cs], in0=x_[:, cs],
                                                                      scalar=float(c.alpha), in1=p[:, 0:w],
                                                                      op0=ALU.mult, op1=ALU.add),
                                  R=[x_tk, p_tk], W=[yacc_tk[sbt]])
                    for e in range(NE):
                        h_, h_tk = hp.next()
                        kb.dma(SP, h_[:], hT[e * EC:(e + 1) * EC, :, t0:t0 + TB].rearrange("j p t -> p j t"),
                               R=[tk["hT"]], W=[h_tk])
                        wd, wd_tk = wdp.next()
                        for j in range(EC):
                            ws, ws_tk = wsp.next()
                            kb.dma(SP, ws[:], w_dn[l, e, j * 128:(j + 1) * 128, :], W=[ws_tk])
                            kb.op(POOL, lambda j=j, ws=ws: G.tensor_copy(out=wd[:, j, :], in_=ws[:]), R=[ws_tk],
                                  W=[wd_tk])
                        for sbt in range(NSB):
                            for cb in range(NB):
                                w = min(512, D - cb * 512)
                                cs = slice(cb * 512, cb * 512 + w)
                                p, p_tk = pm.next()
                                kb.mm(p[:, 0:w], [(h_[:, j, sbt * 128:(sbt + 1) * 128], wd[:, j, cs])
                                                  for j in range(EC)], R=[h_tk, wd_tk], W=[p_tk])
                                kb.op(DVE, lambda: V.tensor_tensor(out=yacc[:, sbt, cs], in0=p[:, 0:w],
                                                                   in1=yacc[:, sbt, cs], op=ALU.add),
                                      R=[p_tk, yacc_tk[sbt]], W=[yacc_tk[sbt]])
                    for sbt in range(NSB):
                        layer_norm(es, yacc[:, sbt, :], yacc_tk[sbt], g, b, gb_tk, stats, mv, st_tk,
                                   yacc[:, sbt, :], yacc_tk[sbt])
                        kb.dma(ACT, dst[t0 + sbt * 128:t0 + (sbt + 1) * 128, :], yacc[:, sbt, :],
                               R=[yacc_tk[sbt]], W=[dst_tk])
                kb.barrier()

        cur, cur_tk = x_in, TK()
        for l in range(L):
            on = lambda p: phases is None or p in phases
            if on("proj"):
                phase_proj(l, cur, cur_tk)
            if on("gdn"):
                phase_gdn(l)
            if on("attn"):
                phase_attn(l)
            if on("merge"):
                phase_merge(l, cur, cur_tk, xa, tk["xa"])
            last = l == L - 1
            dst, dst_tk = (y_out, tk["y"]) if last else (xb, tk["xb"])
            if on("moe"):
                phase_moe(l, xa, tk["xa"], dst, dst_tk)
            cur, cur_tk = xb, tk["xb"]
        kb.barrier()
    if ret_rec:
        return nc, kb.rec
    return nc


def build_two_pass(cfg, **kw):
    _, rec = build_program(cfg, ret_rec=True, **kw)
    return build_program(cfg, needed=rec, **kw)


def host_consts(cfg):
    c = cfg
    S = c.S
    i = np.arange(128)
    ident = np.eye(128, dtype=np.float32)
    ones = np.ones((128, 128), np.float32)
    maskS = np.where(i[:, None] > i[None, :], 0.0, NEG).astype(np.float32)
    maskT = np.where(i[:, None] <= i[None, :], 0.0, NEG).astype(np.float32)
    rm = np.zeros((128, 128), np.float32)
    for m in range(64):
        rm[m + 64, m] = -1.0
    for m in range(64, 128):
        rm[m - 64, m] = 1.0
    sel127 = np.zeros((128, 128), np.float32)
    sel127[127, :] = 1.0
    consts = np.concatenate([ident, ones, maskS, maskT, rm, sel127], axis=1)
    amask = np.concatenate([(i[:, None] >= i[None, :]), (i[:, None] <= i[None, :])], axis=1).astype(np.float32)
    half = 64
    inv_freq = (10000.0 ** (-np.arange(half, dtype=np.float32) / half)).astype(np.float32)
    ang = np.arange(S, dtype=np.float32)[None, :] * np.concatenate([inv_freq, inv_freq])[:, None]
    cos_t = np.cos(ang).astype(np.float32)
    sin_t = np.sin(ang).astype(np.float32)
    scanmask = np.ones((c.DNH, S), np.float32)
    scanmask[:, ::128] = 0.0
    sele = np.zeros((c.NE, c.NE * 128), np.float32)
    for e in range(c.NE):
        sele[e, e * 128:(e + 1) * 128] = 1.0
    return dict(consts=consts, amask=amask, cos_t=cos_t, sin_t=sin_t, scanmask=scanmask, sele=sele)


def host_layout(cfg, inp):
    c = cfg
    L = c.L
    out = {}
    f = lambda a: np.ascontiguousarray(np.asarray(a, dtype=np.float32))
    out["w_in"] = f(inp["w_in"])
    cw = np.asarray(inp["conv_w"], np.float32)
    out["conv_w"] = f(cw.transpose(0, 2, 1).reshape(L, 3 * c.DNH, 128, 4).transpose(0, 2, 1, 3))
    out["a_log"] = f(np.asarray(inp["a_log"]).reshape(L, c.DNH, 1))
    out["dt_bias"] = f(np.asarray(inp["dt_bias"]).reshape(L, c.DNH, 1))
    out["dn_norm_w"] = f(inp["dn_norm_w"])
    for k in ("w_branch_a", "w_branch_b", "w_out", "ln1_g", "ln1_b", "router_w", "router_b", "b_down", "ln2_g",
              "ln2_b"):
        out[k] = f(inp[k])
    out["w_gate_up"] = f(inp["w_gate_up"])
    out["w_down"] = f(inp["w_down"])
    bg = np.asarray(inp["b_gate_up"], np.float32)
    EC2 = 2 * c.DE // 128
    out["b_gate_up"] = f(bg.reshape(L, c.NE, EC2, 128).transpose(0, 3, 1, 2).reshape(L, 128, c.NE * EC2))
    return out


_CACHE = {}


def kernel(**inputs):
    cfg = Cfg()
    x = np.asarray(inputs["x"], np.float32)
    B = x.shape[0]
    shared = host_layout(cfg, inputs)
    shared.update(host_consts(cfg))
    if "nc" not in _CACHE:
        _CACHE["nc"] = build_two_pass(cfg)
    nc = _CACHE["nc"]
    in_maps = []
    for b in range(B):
        m = dict(shared)
        m["x"] = np.ascontiguousarray(x[b])
        in_maps.append(m)
    res = run_bass_kernel_spmd(nc, in_maps, core_ids=list(range(B)))
    return np.stack([np.asarray(r["y"], np.float32) for r in res.results], axis=0)
```
